# Optimizing a Trainium2 kernel written in Bass

```python
import jax
import jax.numpy as jnp
from jax import lax
import numpy as np

D_MODEL = 2048
BATCH = 4
SEQ = 4096
DEPTH = 2

CHUNK = 64
MIX_WIDTH = 2 * D_MODEL
POOL_WIDTH = MIX_WIDTH // 4
N_POOL_GROUPS = 4
POOL_GROUP_DIM = POOL_WIDTH // N_POOL_GROUPS
POOL_WINDOWS = (2, 4, 8, 16)
SB_WIDTH = MIX_WIDTH // 2
SB_HEAD_DIM = 128
SB_HEADS = SB_WIDTH // SB_HEAD_DIM
Q_BLOCK = 128
CONV_CH = MIX_WIDTH // 4
CONV_WIDTH = 3
N_BRANCHES = 3
IN_PROJ_DIM = 2 * POOL_WIDTH + 4 * SB_WIDTH + 4 * CONV_CH
RMS_EPS = 1e-6

kernel_name = "hybrid_pool_stickbreak_shortconv_trunk"


def rms_norm(x, g):
    xf = x.astype(jnp.float32)
    y = xf * lax.rsqrt(jnp.mean(xf * xf, axis=-1, keepdims=True) + RMS_EPS)
    return (y * g.astype(jnp.float32)).astype(x.dtype)


def split_input_projection(proj):
    sizes = (POOL_WIDTH, POOL_WIDTH,
             SB_WIDTH, SB_WIDTH, SB_WIDTH, SB_WIDTH,
             CONV_CH, CONV_CH, CONV_CH, CONV_CH)
    parts, off = [], 0
    for n in sizes:
        parts.append(proj[..., off:off + n])
        off += n
    return parts


def multiscale_pool(xa, pool_w, pool_scale):
    b, s, _ = xa.shape
    xg = xa.astype(jnp.float32).reshape(b, s, N_POOL_GROUPS, POOL_GROUP_DIM)
    cs = jnp.pad(jnp.cumsum(xg, axis=1), ((0, 0), (1, 0), (0, 0), (0, 0)))
    t = jnp.arange(s)
    pooled = []
    for g, w in enumerate(POOL_WINDOWS):
        start = jnp.maximum(t + 1 - w, 0)
        count = (t + 1 - start).astype(jnp.float32)
        win_sum = cs[:, 1:, g] - cs[:, start, g]
        pooled.append(win_sum / count[None, :, None])
    mixed = jnp.stack(pooled, axis=2) - xg
    y = jnp.einsum('bsgc,gcd->bsgd', mixed, pool_w.astype(jnp.float32))
    return (y.reshape(b, s, POOL_WIDTH) * pool_scale).astype(xa.dtype)


def stick_breaking_attention(q, k, v):
    b, s, h, dh = q.shape
    qf = q.astype(jnp.float32) * (dh ** -0.5)
    kf = k.astype(jnp.float32)
    vf = v.astype(jnp.float32)
    outs = []
    for i in range(s // Q_BLOCK):
        q0 = i * Q_BLOCK
        kv_len = q0 + Q_BLOCK
        z = jnp.einsum('bqhd,bkhd->bhqk', qf[:, q0:kv_len], kf[:, :kv_len])
        t_idx = q0 + jnp.arange(Q_BLOCK)[:, None]
        s_idx = jnp.arange(kv_len)[None, :]
        strict = s_idx < t_idx
        log_keep = jnp.where(strict, -jax.nn.softplus(z), 0.0)
        between = lax.cumsum(log_keep, axis=3, reverse=True) - log_keep
        log_a = jax.nn.log_sigmoid(z) + between
        a = jnp.where(strict, jnp.exp(log_a), 0.0)
        outs.append(jnp.einsum('bhqk,bkhd->bqhd', a, vf[:, :kv_len]))
    return jnp.concatenate(outs, axis=1).astype(q.dtype)


def short_gated_conv(u, b_gate, c_gate, conv_w):
    s = u.shape[1]
    v = c_gate * u
    vp = jnp.pad(v, ((0, 0), (CONV_WIDTH - 1, 0), (0, 0)))
    y = conv_w[0] * vp[:, 0:s]
    for i in range(1, CONV_WIDTH):
        y = y + conv_w[i] * vp[:, i:i + s]
    return b_gate * y


def hybrid_layer(x, c, norm_g, w_ada, b_ada, w_in, pool_w, pool_scale, conv_w,
                 w_br_a, w_br_b, w_br_c, w_gate, b_gate, w_out):
    b, s, _ = x.shape
    mod = jax.nn.silu(c) @ w_ada + b_ada
    shift, scale, res_gate = jnp.split(mod, 3, axis=-1)
    h = rms_norm(x, norm_g) * (1.0 + scale[:, None, :]) + shift[:, None, :]
    xa, za, q, k, v, zb, u, bg, cg, zc = split_input_projection(h @ w_in)
    ya = multiscale_pool(xa, pool_w, pool_scale) * jax.nn.silu(za)
    hs = (b, s, SB_HEADS, SB_HEAD_DIM)
    yb = stick_breaking_attention(q.reshape(hs), k.reshape(hs), v.reshape(hs))
    yb = yb.reshape(b, s, SB_WIDTH) * jax.nn.silu(zb)
    yc = short_gated_conv(u, bg, cg, conv_w) * jax.nn.silu(zc)
    gates = jax.nn.sigmoid((h @ w_gate + b_gate).astype(jnp.float32)).astype(x.dtype)
    gates = gates.reshape(b, s, N_BRANCHES, D_MODEL)
    merged = (gates[:, :, 0] * (ya @ w_br_a)
              + gates[:, :, 1] * (yb @ w_br_b)
              + gates[:, :, 2] * (yc @ w_br_c))
    return x + res_gate[:, None, :] * (merged @ w_out)


def setup_inputs(seed: int = 0) -> dict:
    key = jax.random.key(seed)
    ks = jax.random.split(key, 18)

    def nrm(k, shape, scale):
        return scale * jax.random.normal(k, shape, jnp.float32)

    return {
        "x": nrm(ks[0], (BATCH, SEQ, D_MODEL), 1.0),
        "c": nrm(ks[1], (BATCH, D_MODEL), 1.0),
        "norm_g": 1.0 + nrm(ks[2], (DEPTH, D_MODEL), 0.1),
        "w_ada": nrm(ks[3], (DEPTH, D_MODEL, 3 * D_MODEL), D_MODEL ** -0.5),
        "b_ada": nrm(ks[4], (DEPTH, 3 * D_MODEL), 0.02),
        "w_in": nrm(ks[5], (DEPTH, D_MODEL, IN_PROJ_DIM), D_MODEL ** -0.5),
        "pool_w": nrm(ks[6], (DEPTH, N_POOL_GROUPS, POOL_GROUP_DIM, POOL_GROUP_DIM), POOL_GROUP_DIM ** -0.5),
        "pool_scale": 1.0 + nrm(ks[7], (DEPTH, POOL_WIDTH), 0.1),
        "conv_w": nrm(ks[8], (DEPTH, CONV_WIDTH, CONV_CH), CONV_WIDTH ** -0.5),
        "w_br_a": nrm(ks[9], (DEPTH, POOL_WIDTH, D_MODEL), POOL_WIDTH ** -0.5),
        "w_br_b": nrm(ks[10], (DEPTH, SB_WIDTH, D_MODEL), SB_WIDTH ** -0.5),
        "w_br_c": nrm(ks[11], (DEPTH, CONV_CH, D_MODEL), CONV_CH ** -0.5),
        "w_gate": nrm(ks[12], (DEPTH, D_MODEL, N_BRANCHES * D_MODEL), D_MODEL ** -0.5),
        "b_gate": nrm(ks[13], (DEPTH, N_BRANCHES * D_MODEL), 0.1),
        "w_out": nrm(ks[14], (DEPTH, D_MODEL, D_MODEL), D_MODEL ** -0.5),
        "final_g": 1.0 + nrm(ks[15], (D_MODEL,), 0.1),
    }


def reference(x, c, norm_g, w_ada, b_ada, w_in, pool_w, pool_scale, conv_w,
              w_br_a, w_br_b, w_br_c, w_gate, b_gate, w_out, final_g):
    for l in range(DEPTH):
        x = hybrid_layer(x, c, norm_g[l], w_ada[l], b_ada[l], w_in[l], pool_w[l],
                         pool_scale[l], conv_w[l], w_br_a[l], w_br_b[l], w_br_c[l],
                         w_gate[l], b_gate[l], w_out[l])
    return rms_norm(x, final_g)
```

```python
import contextlib
import numpy as np
import concourse.bass as bass
import concourse.mybir as mybir
from concourse.bass_utils import run_bass_kernel_spmd

F32 = mybir.dt.float32
BF16 = mybir.dt.bfloat16
AF = mybir.ActivationFunctionType
ALU = mybir.AluOpType

N_CORES = 8
ACTIVE_CORES = [0, 1, 4, 5]
POOL_WINDOWS = (2, 4, 8, 16)
RMS_EPS = 1e-6
L = 2

CFG_FULL = dict(D=2048, S=4096, TQ=512, HALF=2048)


class Dims:
    def __init__(self, D, S, TQ, HALF):
        self.D, self.S, self.TQ, self.HALF = D, S, TQ, HALF
        self.DC = D // 128
        self.PW = D // 2
        self.PWC = self.PW // 128
        self.GC = self.PWC // 4
        self.CW = D // 2
        self.CWC = self.CW // 128
        self.NTH = HALF // TQ
        self.NH = S // HALF
        self.KB = TQ // 128
        self.NG = 10 * self.DC
        DC = self.DC
        self.o_ng = 0
        self.o_bada = self.o_ng + DC
        self.o_ps = self.o_bada + 3 * DC
        self.o_cw = self.o_ps + self.PWC
        self.o_bg = self.o_cw + 3 * self.CWC
        self.NV = self.o_bg + 3 * DC

    def groups(self):
        D, PW, CW, DC, GC, CWC = self.D, self.PW, self.CW, self.DC, self.GC, self.CWC
        o_xa, o_za = 0, PW
        o_q = 2 * PW
        o_k, o_v, o_zb = o_q + D, o_q + 2 * D, o_q + 3 * D
        o_u = o_q + 4 * D
        o_bgt, o_cg, o_zc = o_u + CW, o_u + 2 * CW, o_u + 3 * CW
        gl = []
        for g in range(4):
            for gi in range(GC):
                gl.append(("xa", g * GC + gi, "in", o_xa + (g * GC + gi) * 128))
            for gi in range(GC):
                gl.append(("za", g * GC + gi, "in", o_za + (g * GC + gi) * 128))
        for i in range(CWC):
            gl.append(("cg", i, "in", o_cg + i * 128))
            gl.append(("u", i, "in", o_u + i * 128))
            gl.append(("bg", i, "in", o_bgt + i * 128))
            gl.append(("zc", i, "in", o_zc + i * 128))
        for hh in range(DC):
            gl.append(("zb", hh, "in", o_zb + hh * 128))
        for hh in range(DC):
            gl.append(("q", hh, "in", o_q + hh * 128))
        for hh in range(DC):
            gl.append(("k", hh, "in", o_k + hh * 128))
        for hh in range(DC):
            gl.append(("v", hh, "in", o_v + hh * 128))
        for c in range(DC):
            gl.append(("g0", c, "gate", 0 * D + c * 128))
            gl.append(("g2", c, "gate", 2 * D + c * 128))
            gl.append(("g1", c, "gate", 1 * D + c * 128))
        assert len(gl) == self.NG
        return gl


class Buf:
    __slots__ = ("name", "w", "r")

    def __init__(self, name):
        self.name = name
        self.w = None
        self.r = {}


class Eng:
    def __init__(self, name, handle, sem):
        self.name, self.h, self.sem = name, handle, sem
        self.cnt = 0
        self.waited = {}


class Sched:
    def __init__(self, nc, stack):
        self.nc = nc
        self.stack = stack
        self.sems = {}
        self.engs = {}
        for name, h in (("pe", nc.tensor), ("act", nc.scalar), ("dve", nc.vector),
                        ("pool", nc.gpsimd), ("sp", nc.sync)):
            sem = stack.enter_context(nc.semaphore("sem_" + name))
            self.sems["E" + name] = sem
            self.engs[name] = Eng(name, h, sem)
        self.dcount = {}
        self.bufs = []

    def buf(self, name):
        b = Buf(name)
        self.bufs.append(b)
        return b

    def _dsem(self, tag):
        key = "D" + tag
        if key not in self.sems:
            self.sems[key] = self.stack.enter_context(self.nc.semaphore("dsem_" + tag))
            self.dcount[key] = 0
        return key

    def _wait(self, e, semkey, val):
        if e.waited.get(semkey, 0) >= val:
            return
        e.h.wait_ge(self.sems[semkey], val)
        e.waited[semkey] = val

    def _deps(self, e, reads, writes):
        deps = {}

        def add(tok):
            if tok is None:
                return
            k, v, en = tok
            if en == "pe" and e.name == "pe":
                return
            if deps.get(k, 0) < v:
                deps[k] = v

        for b in reads:
            add(b.w)
        for b in writes:
            add(b.w)
            for k, (v, en) in b.r.items():
                add((k, v, en))
        for k, v in deps.items():
            self._wait(e, k, v)

    def _mark(self, tok, reads, writes):
        k, v, en = tok
        for b in writes:
            b.w = tok
            b.r = {}
        for b in reads:
            if b.r.get(k, (0, None))[0] < v:
                b.r[k] = (v, en)

    def op(self, eng, fn, reads=(), writes=(), inc=True):
        e = self.engs[eng]
        self._deps(e, reads, writes)
        ins = fn(e.h)
        if inc:
            e.cnt += 1
            ins.then_inc(e.sem, 1)
            tok = ("E" + eng, e.cnt, eng)
        else:
            tok = ("E" + eng, e.cnt + 1, eng)
        self._mark(tok, reads, writes)
        return ins

    def dma(self, q, out, in_, reads=(), writes=(), tag=None):
        e = self.engs[q]
        self._deps(e, reads, writes)
        key = self._dsem(tag or (writes[0].name if writes else reads[0].name))
        ins = e.h.dma_start(out=out, in_=in_)
        self.dcount[key] += 16
        ins.then_inc(self.sems[key], 16)
        tok = (key, self.dcount[key], "dma")
        self._mark(tok, reads, writes)
        return ins

    def barrier(self):
        for e in self.engs.values():
            for e2 in self.engs.values():
                if e2 is not e and e2.cnt > 0:
                    self._wait(e, "E" + e2.name, e2.cnt)
            for key, cnt in self.dcount.items():
                if cnt > 0:
                    self._wait(e, key, cnt)
        for b in self.bufs:
            b.w = None
            b.r = {}


def build_program(cfg):
    dm = Dims(**cfg)
    D, S, TQ, HALF = dm.D, dm.S, dm.TQ, dm.HALF
    DC, PWC, GC, CWC, NTH, NH, KB, NG, NV = dm.DC, dm.PWC, dm.GC, dm.CWC, dm.NTH, dm.NH, dm.KB, dm.NG, dm.NV
    NBLK = S // 128
    HB = HALF // 128
    groups = dm.groups()

    nc = bass.Bass("TRN2", target_bir_lowering=False)

    def din(name, shape, dt=F32):
        return nc.dram_tensor(name, list(shape), dt, kind="ExternalInput").ap()

    NTT = S // TQ
    x_in = din("x", [NTT, 128, DC, TQ])
    cvec_in = din("cvec", [128, DC])
    w_ada_in = din("w_ada", [L, 3 * DC, 128, DC, 128])
    w_fm_in = din("w_fm", [L, NG, 128, DC, 128])
    w_pool_in = din("w_pool", [L, 4, 128, GC, GC, 128])
    w_a_in = din("w_a", [L, DC, 128, PWC, 128])
    w_c_in = din("w_c", [L, DC, 128, CWC, 128])
    w_b_in = din("w_b", [L, DC, 128, DC, 128])
    w_o_in = din("w_o", [L, DC, 128, DC, 128])
    vecs_in = din("vecs", [128, L, NV])
    fing_in = din("fin_g", [128, DC])
    rcnt_in = din("rcnt", [128, 4, 16])
    cmat_in = din("cmat", [128, 2, 128])
    out_d = nc.dram_tensor("out", [NTT, 128, DC, TQ], F32, kind="ExternalOutput").ap()

    def dscr(name, shape, dt):
        return nc.dram_tensor(name, list(shape), dt).ap()

    qT_s = dscr("qT_s", [DC, 128, S], BF16)
    kT_s = dscr("kT_s", [DC, 128, S], BF16)
    V_s = dscr("V_s", [DC, 128, NBLK * 128], BF16)
    szb_s = dscr("szb_s", [DC, 128, S], BF16)
    g1_s = dscr("g1_s", [DC, 128, S], BF16)
    acc_s = dscr("acc_s", [DC, 128, S], F32)
    x1_s = dscr("x1_s", [NTT, 128, DC, TQ], F32)
    x2_s = x1_s

    XH = 16 + HALF
    N1 = max(DC * HALF, 2 * (2 * S + 2 * HALF), 3 * 2 * DC * 128)
    N2 = max(DC * HALF, 3 * DC * TQ)

    with contextlib.ExitStack() as st:
        sc = Sched(nc, st)

        def sb(name, shape, dt):
            return st.enter_context(nc.sbuf_tensor(name, list(shape), dt))

        def ps(name):
            return st.enter_context(nc.psum_tensor(name, [128, 512], F32))

        R1 = sb("R1", [128, N1], BF16)
        R2 = sb("R2", [128, N2], BF16)
        F0 = sb("F0", [128, XH], F32)
        F1 = sb("F1", [128, XH], F32)
        F2 = sb("F2", [128, XH], F32)
        Bm = sb("Bm", [128, 2 * HALF], BF16)
        STG = [sb("STG%d" % i, [128, HALF], BF16) for i in range(2)]
        TMPB = [sb("TMPB%d" % i, [128, TQ], BF16) for i in range(2)]
        TMPF = [sb("TMPF%d" % i, [128, TQ], F32) for i in range(2)]
        WS = [sb("WS%d" % i, [128, DC, 128], BF16) for i in range(3)]
        WP = sb("WP", [128, GC, GC, 128], BF16)
        VEC = sb("VEC", [128, L, NV], F32)
        MOD = sb("MOD", [128, L, 3 * DC], F32)
        GS = sb("GS", [128, L, DC], F32)
        FING = sb("FING", [128, DC], F32)
        SCV = sb("SCV", [128, DC], F32)
        RCNT = sb("RCNT", [128, 4, 16], F32)
        NEGL = sb("NEGL", [128, 128], BF16)
        MASK = sb("MASK", [128, 128], F32)
        ONES = sb("ONES", [128, 128], BF16)
        NEGONES = sb("NEGONES", [128, 128], BF16)
        HXA = sb("HXA", [128, PWC, 16], F32)
        HV = sb("HV", [128, CWC, 2], F32)
        RSTD = sb("RSTD", [128, TQ], F32)
        T16 = sb("T16", [128, 16], F32)
        PS = [ps("PS%d" % i) for i in range(8)]

        b_R1 = sc.buf("R1")
        b_R2 = sc.buf("R2")
        b_F0, b_F1, b_F2 = sc.buf("F0"), sc.buf("F1"), sc.buf("F2")
        b_Bm = sc.buf("Bm")
        b_STG = [sc.buf("STG0"), sc.buf("STG1")]
        b_TMPB = [sc.buf("TMPB0"), sc.buf("TMPB1")]
        b_TMPF = [sc.buf("TMPF0"), sc.buf("TMPF1")]
        b_WS = [sc.buf("WS%d" % i) for i in range(3)]
        b_WP = sc.buf("WP")
        b_small = sc.buf("small")
        b_HXA, b_HV = sc.buf("HXA"), sc.buf("HV")
        b_RSTD = sc.buf("RSTD")
        b_T16 = sc.buf("T16")
        b_PS = [sc.buf("PS%d" % i) for i in range(8)]
        b_qT, b_kT, b_V, b_szb = sc.buf("qT"), sc.buf("kT"), sc.buf("V"), sc.buf("szb")
        b_g1, b_acc = sc.buf("g1"), sc.buf("acc")
        b_x1, b_x2, b_out = sc.buf("x1"), sc.buf("x2"), sc.buf("outd")

        ws_ctr = [0]

        def next_ws():
            i = ws_ctr[0] % 3
            ws_ctr[0] += 1
            return i

        psA_ctr = [0]

        def next_psA():
            i = psA_ctr[0] % 4
            psA_ctr[0] += 1
            return i

        sc.dma("sp", VEC[:], vecs_in, writes=[b_small], tag="small")
        sc.dma("sp", FING[:], fing_in, writes=[b_small], tag="small")
        sc.dma("sp", SCV[:], cvec_in, writes=[b_small], tag="small")
        sc.dma("sp", RCNT[:], rcnt_in, writes=[b_small], tag="small")
        sc.dma("sp", MASK[:], cmat_in[:, 1, :], writes=[b_small], tag="small")
        sc.dma("pool", NEGL[:], cmat_in[:, 0, :], writes=[b_small], tag="small")
        sc.op("dve", lambda e: e.memset(ONES[:], 1.0), writes=[b_small])
        sc.op("dve", lambda e: e.memset(NEGONES[:], -1.0), writes=[b_small])
        sc.op("act", lambda e: e.activation(out=SCV[:], in_=SCV[:], func=AF.Silu),
              reads=[b_small], writes=[b_small])

        WA = [R1[:, i * 2 * DC * 128:(i + 1) * 2 * DC * 128].bitcast(F32).rearrange(
            "p (c j) -> p c j", c=DC) for i in range(3)]
        b_WA = [sc.buf("WA%d" % i) for i in range(3)]
        for l in range(L):
            for g in range(3 * DC):
                i = (l * 3 * DC + g) % 3
                sc.dma("sp", WA[i], w_ada_in[l, g], writes=[b_WA[i]], tag="wa%d" % i)
                for kc in range(DC):
                    sc.op("pe", lambda e, i=i, kc=kc, g=g: e.matmul(
                        PS[0][:, g:g + 1], lhsT=WA[i][:, kc, :], rhs=SCV[:, kc:kc + 1],
                        start=(kc == 0), stop=(kc == DC - 1)),
                        reads=[b_WA[i], b_small], writes=[b_PS[0]], inc=(kc == DC - 1))
            sc.op("dve", lambda e, l=l: e.tensor_tensor(
                out=MOD[:, l, :], in0=PS[0][:, 0:3 * DC],
                in1=VEC[:, l, dm.o_bada:dm.o_bada + 3 * DC], op=ALU.add),
                reads=[b_PS[0], b_small], writes=[b_small])
            sc.op("dve", lambda e, l=l: e.scalar_tensor_tensor(
                out=GS[:, l, :], in0=MOD[:, l, DC:2 * DC], scalar=1.0,
                in1=VEC[:, l, dm.o_ng:dm.o_ng + DC], op0=ALU.add, op1=ALU.mult),
                reads=[b_small], writes=[b_small])
        sc.barrier()

        hT = R1[:, 0:DC * HALF].rearrange("p (c t) -> p c t", c=DC)
        merged = hT
        xt = R2[:, 0:2 * DC * TQ].bitcast(F32).rearrange("p (c t) -> p c t", c=DC)
        sq = R2[:, 2 * DC * TQ:3 * DC * TQ].rearrange("p (c t) -> p c t", c=DC)
        ya = R2[:, 0:PWC * HALF].rearrange("p (c t) -> p c t", c=PWC)
        yc = R2[:, PWC * HALF:DC * HALF].rearrange("p (c t) -> p c t", c=CWC)
        yb = R2[:, 0:DC * HALF].rearrange("p (c t) -> p c t", c=DC)
        mixed = Bm[:, :].rearrange("p (c t) -> p c t", c=2)
        CGt = Bm[:, 0:HALF]
        SG0 = Bm[:, 0:HALF]
        SG2 = Bm[:, HALF:2 * HALF]

        def norm_tile(src, t0, gvec, shift, dst_fn, final_dst=None):
            sc.dma("sp", xt, src[t0 // TQ],
                   reads=[], writes=[b_R2], tag="xt")
            n4 = max(1, DC // 4)
            for c0 in range(0, DC, n4):
                sc.op("act", lambda e, c0=c0: e.activation(
                    out=sq[:, c0:c0 + n4, :], in_=xt[:, c0:c0 + n4, :], func=AF.Square),
                    reads=[b_R2], writes=[b_R2])
            for c in range(DC):
                sc.op("pe", lambda e, c=c: e.matmul(
                    PS[4][:, 0:TQ], lhsT=ONES[:], rhs=sq[:, c, :],
                    start=(c == 0), stop=(c == DC - 1)),
                    reads=[b_R2, b_small], writes=[b_PS[4]], inc=(c == DC - 1))
            sc.op("dve", lambda e: e.tensor_scalar(
                out=RSTD[:], in0=PS[4][:, 0:TQ], scalar1=1.0 / D, scalar2=RMS_EPS,
                op0=ALU.mult, op1=ALU.add), reads=[b_PS[4]], writes=[b_RSTD])
            sc.op("act", lambda e: e.activation(out=RSTD[:], in_=RSTD[:], func=AF.Ln),
                  reads=[b_RSTD], writes=[b_RSTD])
            sc.op("act", lambda e: e.activation(out=RSTD[:], in_=RSTD[:], func=AF.Exp, scale=-0.5),
                  reads=[b_RSTD], writes=[b_RSTD])
            for c in range(DC):
                sc.op("dve", lambda e, c=c: e.scalar_tensor_tensor(
                    out=xt[:, c, :], in0=xt[:, c, :], scalar=gvec[:, c:c + 1], in1=RSTD[:],
                    op0=ALU.mult, op1=ALU.mult), reads=[b_R2, b_RSTD, b_small], writes=[b_R2])
                if dst_fn is not None:
                    sc.op("act", lambda e, c=c: e.activation(
                        out=dst_fn(c), in_=xt[:, c, :], func=AF.Identity,
                        bias=shift[:, c:c + 1], scale=1.0),
                        reads=[b_R2, b_small], writes=[b_R1])
            if final_dst is not None:
                sc.dma("sp", final_dst[t0 // TQ], xt,
                       reads=[b_R2], writes=[b_out], tag="outd")

        def load_w(src_ap, nk):
            i = next_ws()
            sc.dma("pool", WS[i][:, 0:nk, :], src_ap, writes=[b_WS[i]], tag="ws%d" % i)
            return i

        def mm_tile(wi, nk, rhs_fn, rhs_bufs, j):
            pi = next_psA()
            for kc in range(nk):
                sc.op("pe", lambda e, kc=kc: e.matmul(
                    PS[pi][:, 0:TQ], lhsT=WS[wi][:, kc, :], rhs=rhs_fn(kc, j),
                    start=(kc == 0), stop=(kc == nk - 1)),
                    reads=[b_WS[wi]] + rhs_bufs, writes=[b_PS[pi]], inc=(kc == nk - 1))
            return pi

        def tsl(j):
            return slice(j * TQ, (j + 1) * TQ)

        stg_ctr = [0]
        tmp_ctr = [0]

        for l in range(L):
            xin = x_in if l == 0 else x1_s
            b_xin = None if l == 0 else b_x1
            xout = x1_s if l == 0 else x2_s
            b_xout = b_x1 if l == 0 else b_x2
            shiftv = MOD[:, l, 0:DC]
            rgv = MOD[:, l, 2 * DC:3 * DC]
            gsv = GS[:, l, :]
            psv = VEC[:, l, dm.o_ps:dm.o_ps + PWC]
            cwv = VEC[:, l, dm.o_cw:dm.o_cw + 3 * CWC]
            bgv = VEC[:, l, dm.o_bg:dm.o_bg + 3 * DC]

            for h in range(NH):
                T0 = h * HALF
                hs = slice(T0, T0 + HALF)
                for j in range(NTH):
                    norm_tile(xin, T0 + j * TQ, gsv, shiftv,
                              lambda c, j=j: hT[:, c, tsl(j)])
                sc.barrier()

                hrhs = lambda kc, j: hT[:, kc, tsl(j)]
                for (kind, idx, src, col) in groups:
                    gidx = groups.index((kind, idx, src, col))
                    wi = load_w(w_fm_in[l, gidx], DC)
                    if kind == "xa":
                        g = idx // GC
                        gi = idx % GC
                        w = POOL_WINDOWS[g]
                        if h == 0:
                            sc.op("dve", lambda e: e.memset(F0[:, 0:16], 0.0), writes=[b_F0])
                        else:
                            sc.op("dve", lambda e, idx=idx: e.tensor_copy(out=F0[:, 0:16], in_=HXA[:, idx, :]),
                                  reads=[b_HXA], writes=[b_F0])
                        for j in range(NTH):
                            pi = mm_tile(wi, DC, hrhs, [b_R1], j)
                            sc.op("act", lambda e, pi=pi, j=j: e.activation(
                                out=F0[:, 16 + j * TQ:16 + (j + 1) * TQ], in_=PS[pi][:, 0:TQ], func=AF.Copy),
                                reads=[b_PS[pi]], writes=[b_F0])
                        sc.op("dve", lambda e, idx=idx: e.tensor_copy(out=HXA[:, idx, :], in_=F0[:, HALF:HALF + 16]),
                              reads=[b_F0], writes=[b_HXA])
                        cur, cb = F0, b_F0
                        step = 1
                        k = 0
                        while step < w:
                            nxt, nb = (F1, b_F1) if k % 2 == 0 else (F2, b_F2)
                            sc.op("dve", lambda e, cur=cur, nxt=nxt, step=step: e.tensor_tensor(
                                out=nxt[:, step:XH], in0=cur[:, step:XH], in1=cur[:, 0:XH - step], op=ALU.add),
                                reads=[cb], writes=[nb])
                            cur, cb = nxt, nb
                            step *= 2
                            k += 1
                        sc.op("dve", lambda e, cur=cur, gi=gi, w=w: e.scalar_tensor_tensor(
                            out=mixed[:, gi, :], in0=cur[:, 16:XH], scalar=1.0 / w, in1=F0[:, 16:XH],
                            op0=ALU.mult, op1=ALU.subtract), reads=[cb, b_F0], writes=[b_Bm])
                        if h == 0:
                            sc.op("dve", lambda e, cur=cur, g=g: e.tensor_tensor(
                                out=T16[:], in0=cur[:, 16:32], in1=RCNT[:, g, :], op=ALU.mult),
                                reads=[cb, b_small], writes=[b_T16])
                            sc.op("dve", lambda e, gi=gi: e.tensor_tensor(
                                out=mixed[:, gi, 0:16], in0=T16[:], in1=F0[:, 16:32], op=ALU.subtract),
                                reads=[b_T16, b_F0], writes=[b_Bm])
                        if gi == GC - 1:
                            sc.dma("pool", WP[:], w_pool_in[l, g], writes=[b_WP], tag="wp")
                            for oc in range(GC):
                                ch = g * GC + oc
                                for j in range(NTH):
                                    pi = next_psA()
                                    for kc in range(GC):
                                        sc.op("pe", lambda e, oc=oc, kc=kc, j=j, pi=pi: e.matmul(
                                            PS[pi][:, 0:TQ], lhsT=WP[:, oc, kc, :], rhs=mixed[:, kc, tsl(j)],
                                            start=(kc == 0), stop=(kc == GC - 1)),
                                            reads=[b_WP, b_Bm], writes=[b_PS[pi]], inc=(kc == GC - 1))
                                    sc.op("dve", lambda e, pi=pi, ch=ch, j=j: e.tensor_scalar(
                                        out=ya[:, ch, tsl(j)], in0=PS[pi][:, 0:TQ], scalar1=psv[:, ch:ch + 1],
                                        scalar2=None, op0=ALU.mult), reads=[b_PS[pi], b_small], writes=[b_R2])
                    elif kind == "za":
                        for j in range(NTH):
                            pi = mm_tile(wi, DC, hrhs, [b_R1], j)
                            ti = tmp_ctr[0] % 2
                            tmp_ctr[0] += 1
                            sc.op("act", lambda e, pi=pi, ti=ti: e.activation(
                                out=TMPB[ti][:], in_=PS[pi][:, 0:TQ], func=AF.Silu),
                                reads=[b_PS[pi]], writes=[b_TMPB[ti]])
                            sc.op("dve", lambda e, ti=ti, idx=idx, j=j: e.tensor_tensor(
                                out=ya[:, idx, tsl(j)], in0=ya[:, idx, tsl(j)], in1=TMPB[ti][:], op=ALU.mult),
                                reads=[b_TMPB[ti], b_R2], writes=[b_R2])
                    elif kind == "cg":
                        for j in range(NTH):
                            pi = mm_tile(wi, DC, hrhs, [b_R1], j)
                            sc.op("act", lambda e, pi=pi, j=j: e.activation(
                                out=CGt[:, tsl(j)], in_=PS[pi][:, 0:TQ], func=AF.Copy),
                                reads=[b_PS[pi]], writes=[b_Bm])
                    elif kind == "u":
                        if h == 0:
                            sc.op("dve", lambda e: e.memset(F0[:, 0:2], 0.0), writes=[b_F0])
                        else:
                            sc.op("dve", lambda e, idx=idx: e.tensor_copy(out=F0[:, 0:2], in_=HV[:, idx, :]),
                                  reads=[b_HV], writes=[b_F0])
                        for j in range(NTH):
                            pi = mm_tile(wi, DC, hrhs, [b_R1], j)
                            sc.op("dve", lambda e, pi=pi, j=j: e.tensor_tensor(
                                out=F0[:, 2 + j * TQ:2 + (j + 1) * TQ], in0=PS[pi][:, 0:TQ], in1=CGt[:, tsl(j)],
                                op=ALU.mult), reads=[b_PS[pi], b_Bm], writes=[b_F0])
                        sc.op("dve", lambda e, idx=idx: e.tensor_copy(out=HV[:, idx, :], in_=F0[:, HALF:HALF + 2]),
                              reads=[b_F0], writes=[b_HV])
                        sc.op("dve", lambda e, idx=idx: e.tensor_scalar(
                            out=F1[:, 0:HALF], in0=F0[:, 2:2 + HALF], scalar1=cwv[:, 2 * CWC + idx:2 * CWC + idx + 1],
                            scalar2=None, op0=ALU.mult), reads=[b_F0, b_small], writes=[b_F1])
                        sc.op("dve", lambda e, idx=idx: e.scalar_tensor_tensor(
                            out=F1[:, 0:HALF], in0=F0[:, 1:1 + HALF], scalar=cwv[:, CWC + idx:CWC + idx + 1],
                            in1=F1[:, 0:HALF], op0=ALU.mult, op1=ALU.add), reads=[b_F0, b_F1, b_small], writes=[b_F1])
                        sc.op("dve", lambda e, idx=idx: e.scalar_tensor_tensor(
                            out=F1[:, 0:HALF], in0=F0[:, 0:HALF], scalar=cwv[:, idx:idx + 1],
                            in1=F1[:, 0:HALF], op0=ALU.mult, op1=ALU.add), reads=[b_F0, b_F1, b_small], writes=[b_F1])
                    elif kind == "bg":
                        for j in range(NTH):
                            pi = mm_tile(wi, DC, hrhs, [b_R1], j)
                            sc.op("dve", lambda e, pi=pi, j=j: e.tensor_tensor(
                                out=F1[:, tsl(j)], in0=PS[pi][:, 0:TQ], in1=F1[:, tsl(j)], op=ALU.mult),
                                reads=[b_PS[pi], b_F1], writes=[b_F1])
                    elif kind == "zc":
                        for j in range(NTH):
                            pi = mm_tile(wi, DC, hrhs, [b_R1], j)
                            ti = tmp_ctr[0] % 2
                            tmp_ctr[0] += 1
                            sc.op("act", lambda e, pi=pi, ti=ti: e.activation(
                                out=TMPF[ti][:], in_=PS[pi][:, 0:TQ], func=AF.Silu),
                                reads=[b_PS[pi]], writes=[b_TMPF[ti]])
                            sc.op("dve", lambda e, ti=ti, idx=idx, j=j: e.tensor_tensor(
                                out=yc[:, idx, tsl(j)], in0=TMPF[ti][:], in1=F1[:, tsl(j)], op=ALU.mult),
                                reads=[b_TMPF[ti], b_F1], writes=[b_R2])
                    elif kind in ("q", "k", "zb", "g1"):
                        si = stg_ctr[0] % 2
                        stg_ctr[0] += 1
                        for j in range(NTH):
                            pi = mm_tile(wi, DC, hrhs, [b_R1], j)
                            if kind == "q":
                                fn = lambda e, pi=pi, j=j, si=si: e.activation(
                                    out=STG[si][:, tsl(j)], in_=PS[pi][:, 0:TQ], func=AF.Copy, scale=float(128 ** -0.5))
                                rd = [b_PS[pi]]
                            elif kind == "k":
                                fn = lambda e, pi=pi, j=j, si=si: e.activation(
                                    out=STG[si][:, tsl(j)], in_=PS[pi][:, 0:TQ], func=AF.Copy)
                                rd = [b_PS[pi]]
                            elif kind == "zb":
                                fn = lambda e, pi=pi, j=j, si=si: e.activation(
                                    out=STG[si][:, tsl(j)], in_=PS[pi][:, 0:TQ], func=AF.Silu)
                                rd = [b_PS[pi]]
                            else:
                                fn = lambda e, pi=pi, j=j, si=si, idx=idx: e.activation(
                                    out=STG[si][:, tsl(j)], in_=PS[pi][:, 0:TQ], func=AF.Sigmoid,
                                    bias=bgv[:, DC + idx:DC + idx + 1], scale=1.0)
                                rd = [b_PS[pi], b_small]
                            sc.op("act", fn, reads=rd, writes=[b_STG[si]])
                        dst, db = {"q": (qT_s, b_qT), "k": (kT_s, b_kT), "zb": (szb_s, b_szb),
                                   "g1": (g1_s, b_g1)}[kind]
                        sc.dma("sp", dst[idx, :, hs], STG[si][:], reads=[b_STG[si]], writes=[db],
                               tag="st_" + kind)
                    elif kind == "v":
                        si = stg_ctr[0] % 2
                        stg_ctr[0] += 1
                        nb4 = min(4, HB)
                        for tb0 in range(0, HB, nb4):
                            pi = next_psA()
                            for bi in range(nb4):
                                tb = tb0 + bi
                                for kc in range(DC):
                                    sc.op("pe", lambda e, kc=kc, tb=tb, bi=bi, pi=pi: e.matmul(
                                        PS[pi][:, bi * 128:(bi + 1) * 128], lhsT=hT[:, kc, tb * 128:(tb + 1) * 128],
                                        rhs=WS[wi][:, kc, :], start=(kc == 0), stop=(kc == DC - 1)),
                                        reads=[b_WS[wi], b_R1], writes=[b_PS[pi]], inc=(kc == DC - 1))
                            sc.op("act", lambda e, pi=pi, tb0=tb0, si=si: e.activation(
                                out=STG[si][:, tb0 * 128:(tb0 + nb4) * 128], in_=PS[pi][:, 0:nb4 * 128], func=AF.Copy),
                                reads=[b_PS[pi]], writes=[b_STG[si]])
                        sc.dma("sp", V_s[idx, :, h * HALF:(h + 1) * HALF], STG[si][:], reads=[b_STG[si]],
                               writes=[b_V], tag="st_v")
                    elif kind in ("g0", "g2"):
                        dstt = SG0 if kind == "g0" else SG2
                        boff = 0 if kind == "g0" else 2 * DC
                        for j in range(NTH):
                            pi = mm_tile(wi, DC, hrhs, [b_R1], j)
                            sc.op("act", lambda e, pi=pi, j=j, dstt=dstt, boff=boff, idx=idx: e.activation(
                                out=dstt[:, tsl(j)], in_=PS[pi][:, 0:TQ], func=AF.Sigmoid,
                                bias=bgv[:, boff + idx:boff + idx + 1], scale=1.0),
                                reads=[b_PS[pi], b_small], writes=[b_Bm])
                    if kind == "g1":
                        c = idx
                        AC, b_AC = (F1, b_F1) if c % 2 == 0 else (F2, b_F2)
                        wa = load_w(w_a_in[l, c], PWC)
                        for j in range(NTH):
                            pi = mm_tile(wa, PWC, lambda kc, j: ya[:, kc, tsl(j)], [b_R2], j)
                            sc.op("dve", lambda e, pi=pi, j=j, AC=AC: e.tensor_tensor(
                                out=AC[:, tsl(j)], in0=PS[pi][:, 0:TQ], in1=SG0[:, tsl(j)], op=ALU.mult),
                                reads=[b_PS[pi], b_Bm], writes=[b_AC])
                        wc = load_w(w_c_in[l, c], CWC)
                        for j in range(NTH):
                            pi = mm_tile(wc, CWC, lambda kc, j: yc[:, kc, tsl(j)], [b_R2], j)
                            ti = tmp_ctr[0] % 2
                            tmp_ctr[0] += 1
                            sc.op("dve", lambda e, pi=pi, j=j, ti=ti: e.tensor_tensor(
                                out=TMPF[ti][:], in0=PS[pi][:, 0:TQ], in1=SG2[:, tsl(j)], op=ALU.mult),
                                reads=[b_PS[pi], b_Bm], writes=[b_TMPF[ti]])
                            sc.op("dve", lambda e, j=j, ti=ti, AC=AC: e.tensor_tensor(
                                out=AC[:, tsl(j)], in0=AC[:, tsl(j)], in1=TMPF[ti][:], op=ALU.add),
                                reads=[b_TMPF[ti], b_AC], writes=[b_AC])
                        sc.dma("sp", acc_s[c, :, hs], AC[:, 0:HALF], reads=[b_AC], writes=[b_acc], tag="st_acc")
                sc.barrier()

                nkeys = (h + 1) * HALF
                nkb = nkeys // 128
                per = 2 * S + 2 * HALF
                KT = [R1[:, i * per:i * per + S] for i in range(2)]
                VS = [R1[:, i * per + S:i * per + 2 * S].rearrange("p (b d) -> p b d", d=128) for i in range(2)]
                QS = [R1[:, i * per + 2 * S:i * per + 2 * S + HALF] for i in range(2)]
                ZS = [R1[:, i * per + 2 * S + HALF:i * per + 2 * S + 2 * HALF] for i in range(2)]
                b_KT = [sc.buf("KT%d" % i) for i in range(2)]
                b_VS = [sc.buf("VS%d" % i) for i in range(2)]
                b_QS = [sc.buf("QS%d" % i) for i in range(2)]
                b_ZS = [sc.buf("ZS%d" % i) for i in range(2)]
                SB = 16
                spool = []
                for Fx in (F0, F1, F2):
                    fb = Fx[:, 0:(XH // 2) * 2].bitcast(BF16)
                    for i in range((XH // 2) * 2 * 2 // TQ):
                        spool.append(fb[:, i * TQ:(i + 1) * TQ])
                for i in range(2 * HALF // TQ):
                    spool.append(Bm[:, i * TQ:(i + 1) * TQ])
                spool = spool[:2 * SB]
                NSP = len(spool)
                b_SP = [sc.buf("SP%d" % i) for i in range(NSP)]
                Ab = [STG[0][:, 0:TQ], STG[1][:, 0:TQ], TMPB[1][:, 0:TQ]]
                b_A = [sc.buf("A%d" % i) for i in range(3)]
                RS = TMPB[0][:, 0:TQ]
                b_RS = sc.buf("RS")

                items = []
                for hh in range(DC):
                    for j in range(NTH):
                        qi = h * NTH + j
                        blocks = []
                        for kt in range(qi, -1, -1):
                            for kb in range(KB - 1, -1, -1):
                                blocks.append((kt * KB + kb, kt == qi, kb * 128 if kt == qi else 0))
                        nb = len(blocks)
                        bsz = min(SB, NSP // 2)
                        for b0 in range(0, nb, bsz):
                            items.append(dict(hh=hh, j=j, blocks=blocks, b0=b0, b1=min(nb, b0 + bsz), nb=nb,
                                              unit=hh * NTH + j))
                loaded = set()
                sp_ctr = [0]
                a_ctr = [0]
                pz_ctr = [0]
                pc_ctr = [0]

                def ensure_loaded(hh):
                    if hh in loaded:
                        return
                    loaded.add(hh)
                    bs = hh % 2
                    sc.dma("sp", KT[bs][:, 0:nkeys], kT_s[hh, :, 0:nkeys], writes=[b_KT[bs]], tag="kt%d" % bs)
                    sc.dma("sp", VS[bs][:, 0:nkb, :], V_s[hh, :, 0:nkeys].rearrange("p (b d) -> p b d", d=128),
                           writes=[b_VS[bs]], tag="vs%d" % bs)
                    sc.dma("sp", QS[bs], qT_s[hh, :, hs], writes=[b_QS[bs]], tag="qs%d" % bs)
                    sc.dma("sp", ZS[bs], szb_s[hh, :, hs], writes=[b_ZS[bs]], tag="zs%d" % bs)

                def S1(it):
                    hh, j, bs = it["hh"], it["j"], it["hh"] % 2
                    ensure_loaded(hh)
                    it["slots"] = []
                    for i in range(it["b0"], it["b1"]):
                        gb, diag, c0 = it["blocks"][i]
                        slot = sp_ctr[0] % NSP
                        sp_ctr[0] += 1
                        it["slots"].append(slot)
                        pz = pz_ctr[0] % 2
                        pz_ctr[0] += 1
                        qsl = QS[bs][:, j * TQ + c0:(j + 1) * TQ]
                        sc.op("pe", lambda e: e.matmul(PS[pz][:, c0:TQ], lhsT=KT[bs][:, gb * 128:(gb + 1) * 128],
                                                       rhs=qsl, start=True, stop=True),
                              reads=[b_KT[bs], b_QS[bs]], writes=[b_PS[pz]])
                        sc.op("act", lambda e: e.activation(out=spool[slot][:, c0:TQ], in_=PS[pz][:, c0:TQ],
                                                            func=AF.Softplus),
                              reads=[b_PS[pz]], writes=[b_SP[slot]])
                        if diag:
                            sc.op("dve", lambda e: e.tensor_tensor(
                                out=spool[slot][:, c0:c0 + 128], in0=spool[slot][:, c0:c0 + 128], in1=MASK[:],
                                op=ALU.mult), reads=[b_SP[slot], b_small], writes=[b_SP[slot]])

                def S2(it):
                    hh, j, bs, nb = it["hh"], it["j"], it["hh"] % 2, it["nb"]
                    pyi = 4 + it["unit"] % 2
                    if it["b0"] == 0:
                        sc.op("pool", lambda e: e.memset(RS, 0.0), writes=[b_RS])
                    pend = None

                    def av(i, ai):
                        gb, diag, c0 = it["blocks"][i]
                        sc.op("pe", lambda e: e.matmul(PS[pyi][:, c0:TQ], lhsT=VS[bs][:, gb, :], rhs=Ab[ai][:, c0:TQ],
                                                       start=(i == 0), stop=(i == nb - 1), skip_group_check=True),
                              reads=[b_VS[bs], b_A[ai]], writes=[b_PS[pyi]], inc=(i == nb - 1))

                    for n, i in enumerate(range(it["b0"], it["b1"])):
                        gb, diag, c0 = it["blocks"][i]
                        slot = it["slots"][n]
                        pc = 2 + pc_ctr[0] % 2
                        pc_ctr[0] += 1
                        ai = a_ctr[0] % 3
                        a_ctr[0] += 1
                        qsl = QS[bs][:, j * TQ + c0:(j + 1) * TQ]
                        first = (i == 0)
                        sc.op("pe", lambda e: e.matmul(PS[pc][:, c0:TQ], lhsT=KT[bs][:, gb * 128:(gb + 1) * 128],
                                                       rhs=qsl, start=True, stop=False),
                              reads=[b_KT[bs], b_QS[bs]], writes=[b_PS[pc]], inc=False)
                        sc.op("pe", lambda e: e.matmul(PS[pc][:, c0:TQ], lhsT=NEGL[:], rhs=spool[slot][:, c0:TQ],
                                                       start=False, stop=first),
                              reads=[b_SP[slot], b_small], writes=[b_PS[pc]], inc=first)
                        if not first:
                            sc.op("pe", lambda e: e.matmul(PS[pc][:, c0:TQ], lhsT=NEGONES[:], rhs=RS[:, c0:TQ],
                                                           start=False, stop=True),
                                  reads=[b_RS, b_small], writes=[b_PS[pc]], inc=True)
                        if i < nb - 1:
                            sc.op("pool", lambda e: e.tensor_tensor(out=RS[:, c0:TQ], in0=RS[:, c0:TQ],
                                                                    in1=spool[slot][:, c0:TQ], op=ALU.add),
                                  reads=[b_SP[slot], b_RS], writes=[b_RS])
                        sc.op("act", lambda e: e.activation(out=Ab[ai][:, c0:TQ], in_=PS[pc][:, c0:TQ], func=AF.Exp),
                              reads=[b_PS[pc]], writes=[b_A[ai]])
                        if diag:
                            sc.op("dve", lambda e: e.tensor_tensor(
                                out=Ab[ai][:, c0:c0 + 128], in0=Ab[ai][:, c0:c0 + 128], in1=MASK[:], op=ALU.mult),
                                reads=[b_A[ai], b_small], writes=[b_A[ai]])
                        if pend is not None:
                            av(*pend)
                        pend = (i, ai)
                    av(*pend)
                    if it["b1"] == nb:
                        sc.op("dve", lambda e: e.tensor_tensor(
                            out=yb[:, hh, tsl(j)], in0=PS[pyi][:, 0:TQ], in1=ZS[bs][:, tsl(j)], op=ALU.mult),
                            reads=[b_PS[pyi], b_ZS[bs]], writes=[b_R2])

                S1(items[0])
                for n in range(len(items)):
                    if n + 1 < len(items):
                        S1(items[n + 1])
                    S2(items[n])
                sc.barrier()

                G1b = [STG[0], STG[1]]
                b_G1 = [sc.buf("G1_0"), sc.buf("G1_1")]
                ACb = [F1, F2]
                b_ACb = [sc.buf("ACb0"), sc.buf("ACb1")]
                for c in range(DC):
                    bs = c % 2
                    wb = load_w(w_b_in[l, c], DC)
                    sc.dma("sp", G1b[bs][:], g1_s[c, :, hs], writes=[b_G1[bs]], tag="g1l%d" % bs)
                    sc.dma("sp", ACb[bs][:, 0:HALF], acc_s[c, :, hs], writes=[b_ACb[bs]], tag="acl%d" % bs)
                    for j in range(NTH):
                        pi = mm_tile(wb, DC, lambda kc, j: yb[:, kc, tsl(j)], [b_R2], j)
                        ti = tmp_ctr[0] % 2
                        tmp_ctr[0] += 1
                        sc.op("dve", lambda e, pi=pi, j=j, ti=ti, bs=bs: e.tensor_tensor(
                            out=TMPF[ti][:], in0=PS[pi][:, 0:TQ], in1=G1b[bs][:, tsl(j)], op=ALU.mult),
                            reads=[b_PS[pi], b_G1[bs]], writes=[b_TMPF[ti]])
                        sc.op("dve", lambda e, j=j, ti=ti, bs=bs, c=c: e.tensor_tensor(
                            out=merged[:, c, tsl(j)], in0=TMPF[ti][:], in1=ACb[bs][:, tsl(j)], op=ALU.add),
                            reads=[b_TMPF[ti], b_ACb[bs]], writes=[b_R1])
                XRb = [F1, F2]
                for c in range(DC):
                    bs = c % 2
                    wo = load_w(w_o_in[l, c], DC)
                    sc.dma("sp", XRb[bs][:, 0:HALF].rearrange("p (n t) -> p n t", t=TQ),
                           xin[h * NTH:(h + 1) * NTH, :, c, :].rearrange("n p t -> p n t"),
                           writes=[b_ACb[bs]], tag="xrl%d" % bs)
                    for j in range(NTH):
                        pi = mm_tile(wo, DC, lambda kc, j: merged[:, kc, tsl(j)], [b_R1], j)
                        sc.op("dve", lambda e, pi=pi, j=j, bs=bs, c=c: e.scalar_tensor_tensor(
                            out=XRb[bs][:, tsl(j)], in0=PS[pi][:, 0:TQ], scalar=rgv[:, c:c + 1],
                            in1=XRb[bs][:, tsl(j)], op0=ALU.mult, op1=ALU.add),
                            reads=[b_PS[pi], b_ACb[bs], b_small], writes=[b_ACb[bs]])
                    sc.dma("sp", xout[h * NTH:(h + 1) * NTH, :, c, :].rearrange("n p t -> p n t"),
                           XRb[bs][:, 0:HALF].rearrange("p (n t) -> p n t", t=TQ), reads=[b_ACb[bs]],
                           writes=[b_xout], tag="st_x")
                sc.barrier()

        for t in range(S // TQ):
            norm_tile(x2_s, t * TQ, FING, None, None, final_dst=out_d)
        sc.barrier()
    return nc


def _fm_weight(w, K):
    n = w.shape[1] // 128
    return np.ascontiguousarray(w.reshape(K // 128, 128, n, 128).transpose(2, 1, 0, 3))


def _fm_vec(v):
    return np.ascontiguousarray(v.reshape(-1, 128).T)


def prep_inputs(inp, cfg):
    dm = Dims(**cfg)
    D, S, DC = dm.D, dm.S, dm.DC
    groups = dm.groups()
    B = inp["x"].shape[0]
    f32 = np.float32
    w_ada = np.stack([_fm_weight(np.asarray(inp["w_ada"][l], f32), D) for l in range(L)])
    w_fm = []
    for l in range(L):
        wi = np.asarray(inp["w_in"][l], f32)
        wg = np.asarray(inp["w_gate"][l], f32)
        cols = [(wi if src == "in" else wg)[:, col:col + 128] for (_, _, src, col) in groups]
        w_fm.append(_fm_weight(np.concatenate(cols, axis=1), D))
    w_fm = np.stack(w_fm)
    GCd = dm.GC * 128
    pw = np.asarray(inp["pool_w"], f32).reshape(L, 4, dm.GC, 128, dm.GC, 128).transpose(0, 1, 3, 4, 2, 5)
    w_pool = np.ascontiguousarray(pw)
    w_a = np.stack([_fm_weight(np.asarray(inp["w_br_a"][l], f32), dm.PW) for l in range(L)])
    w_c = np.stack([_fm_weight(np.asarray(inp["w_br_c"][l], f32), dm.CW) for l in range(L)])
    w_b = np.stack([_fm_weight(np.asarray(inp["w_br_b"][l], f32), D) for l in range(L)])
    w_o = np.stack([_fm_weight(np.asarray(inp["w_out"][l], f32), D) for l in range(L)])
    vecs = np.zeros((128, L, dm.NV), f32)
    for l in range(L):
        vecs[:, l, dm.o_ng:dm.o_ng + DC] = _fm_vec(np.asarray(inp["norm_g"][l], f32))
        vecs[:, l, dm.o_bada:dm.o_bada + 3 * DC] = _fm_vec(np.asarray(inp["b_ada"][l], f32))
        vecs[:, l, dm.o_ps:dm.o_ps + dm.PWC] = _fm_vec(np.asarray(inp["pool_scale"][l], f32))
        cw = np.asarray(inp["conv_w"][l], f32)
        vecs[:, l, dm.o_cw:dm.o_cw + 3 * dm.CWC] = np.concatenate([_fm_vec(cw[i]) for i in range(3)], axis=1)
        vecs[:, l, dm.o_bg:dm.o_bg + 3 * DC] = _fm_vec(np.asarray(inp["b_gate"][l], f32))
    fin_g = _fm_vec(np.asarray(inp["final_g"], f32))
    rc = np.zeros((128, 4, 16), f32)
    for g, w in enumerate(POOL_WINDOWS):
        rc[:, g, :] = 1.0 / np.minimum(np.arange(16) + 1, w).astype(f32)
    cm = np.zeros((128, 2, 128), f32)
    p = np.arange(128)[:, None]
    jj = np.arange(128)[None, :]
    cm[:, 0, :] = np.where(p >= jj, -1.0, 0.0)
    cm[:, 1, :] = np.where(p < jj, 1.0, 0.0)
    shared = dict(w_ada=w_ada, w_fm=w_fm, w_pool=w_pool, w_a=w_a, w_c=w_c, w_b=w_b, w_o=w_o,
                  vecs=vecs, fin_g=fin_g, rcnt=rc, cmat=cm)
    idle = {k: np.zeros_like(v) for k, v in shared.items()}
    in_maps = []
    x = np.asarray(inp["x"], f32)
    c = np.asarray(inp["c"], f32)
    for core in range(N_CORES):
        if core in ACTIVE_CORES:
            b = ACTIVE_CORES.index(core)
            xT = np.ascontiguousarray(x[b].T.reshape(DC, 128, S // dm.TQ, dm.TQ).transpose(2, 1, 0, 3))
            m = dict(shared)
            m["x"] = xT
            m["cvec"] = _fm_vec(c[b])
        else:
            m = dict(idle)
            m["x"] = np.zeros((S // dm.TQ, 128, DC, dm.TQ), f32)
            m["cvec"] = np.zeros((128, DC), f32)
        in_maps.append(m)
    return in_maps


def assemble(results, cfg, B):
    dm = Dims(**cfg)
    out = np.empty((B, dm.S, dm.D), np.float32)
    for b in range(B):
        o = np.asarray(results[ACTIVE_CORES[b]]["out"], np.float32).reshape(dm.S // dm.TQ, 128, dm.DC, dm.TQ)
        out[b] = o.transpose(0, 3, 2, 1).reshape(dm.S, dm.D)
    return out


_NC_CACHE = {}


def run(inp, cfg):
    key = tuple(sorted(cfg.items()))
    if key not in _NC_CACHE:
        _NC_CACHE[key] = build_program(cfg)
    nc = _NC_CACHE[key]
    in_maps = prep_inputs(inp, cfg)
    res = run_bass_kernel_spmd(nc, in_maps, core_ids=list(range(N_CORES)))
    return assemble(res.results, cfg, inp["x"].shape[0])


def kernel(**inputs):
    return run(inputs, CFG_FULL)
```

```python
import contextlib
import numpy as np
import concourse.bass as bass
import concourse.mybir as mybir
from concourse.bass_utils import run_bass_kernel_spmd

F32 = mybir.dt.float32
BF16 = mybir.dt.bfloat16
AF = mybir.ActivationFunctionType
ALU = mybir.AluOpType

N_CORES = 8
ACTIVE_CORES = [0, 1, 4, 5]
POOL_WINDOWS = (2, 4, 8, 16)
RMS_EPS = 1e-6
L = 2

CFG_FULL = dict(D=2048, S=4096, TQ=512, HALF=2048)


class Dims:
    def __init__(self, D, S, TQ, HALF):
        self.D, self.S, self.TQ, self.HALF = D, S, TQ, HALF
        self.DC = D // 128
        self.PW = D // 2
        self.PWC = self.PW // 128
        self.GC = self.PWC // 4
        self.CW = D // 2
        self.CWC = self.CW // 128
        self.NTH = HALF // TQ
        self.NH = S // HALF
        self.KB = TQ // 128
        self.NG = 10 * self.DC
        DC = self.DC
        self.o_ng = 0
        self.o_bada = self.o_ng + DC
        self.o_ps = self.o_bada + 3 * DC
        self.o_cw = self.o_ps + self.PWC
        self.o_bg = self.o_cw + 3 * self.CWC
        self.NV = self.o_bg + 3 * DC

    def groups(self):
        D, PW, CW, DC, GC, CWC = self.D, self.PW, self.CW, self.DC, self.GC, self.CWC
        o_xa, o_za = 0, PW
        o_q = 2 * PW
        o_k, o_v, o_zb = o_q + D, o_q + 2 * D, o_q + 3 * D
        o_u = o_q + 4 * D
        o_bgt, o_cg, o_zc = o_u + CW, o_u + 2 * CW, o_u + 3 * CW
        gl = []
        for g in range(4):
            for gi in range(GC):
                gl.append(("xa", g * GC + gi, "in", o_xa + (g * GC + gi) * 128))
            for gi in range(GC):
                gl.append(("za", g * GC + gi, "in", o_za + (g * GC + gi) * 128))
        for i in range(CWC):
            gl.append(("cg", i, "in", o_cg + i * 128))
            gl.append(("u", i, "in", o_u + i * 128))
            gl.append(("bg", i, "in", o_bgt + i * 128))
            gl.append(("zc", i, "in", o_zc + i * 128))
        for hh in range(DC):
            gl.append(("zb", hh, "in", o_zb + hh * 128))
        for hh in range(DC):
            gl.append(("q", hh, "in", o_q + hh * 128))
        for hh in range(DC):
            gl.append(("k", hh, "in", o_k + hh * 128))
        for hh in range(DC):
            gl.append(("v", hh, "in", o_v + hh * 128))
        for c in range(DC):
            gl.append(("g0", c, "gate", 0 * D + c * 128))
            gl.append(("g2", c, "gate", 2 * D + c * 128))
            gl.append(("g1", c, "gate", 1 * D + c * 128))
        assert len(gl) == self.NG
        return gl


class Buf:
    __slots__ = ("name", "w", "r")

    def __init__(self, name):
        self.name = name
        self.w = None
        self.r = {}


class Eng:
    def __init__(self, name, handle, sem):
        self.name, self.h, self.sem = name, handle, sem
        self.cnt = 0
        self.waited = {}


class Sched:
    def __init__(self, nc, stack):
        self.nc = nc
        self.stack = stack
        self.sems = {}
        self.engs = {}
        for name, h in (("pe", nc.tensor), ("act", nc.scalar), ("dve", nc.vector),
                        ("pool", nc.gpsimd), ("sp", nc.sync)):
            sem = stack.enter_context(nc.semaphore("sem_" + name))
            self.sems["E" + name] = sem
            self.engs[name] = Eng(name, h, sem)
        self.dcount = {}
        self.bufs = []

    def buf(self, name):
        b = Buf(name)
        self.bufs.append(b)
        return b

    def _dsem(self, tag):
        key = "D" + tag
        if key not in self.sems:
            self.sems[key] = self.stack.enter_context(self.nc.semaphore("dsem_" + tag))
            self.dcount[key] = 0
        return key

    def _wait(self, e, semkey, val):
        if e.waited.get(semkey, 0) >= val:
            return
        e.h.wait_ge(self.sems[semkey], val)
        e.waited[semkey] = val

    def _deps(self, e, reads, writes):
        deps = {}

        def add(tok):
            if tok is None:
                return
            k, v, en = tok
            if en == "pe" and e.name == "pe":
                return
            if deps.get(k, 0) < v:
                deps[k] = v

        for b in reads:
            add(b.w)
        for b in writes:
            add(b.w)
            for k, (v, en) in b.r.items():
                add((k, v, en))
        for k, v in deps.items():
            self._wait(e, k, v)

    def _mark(self, tok, reads, writes):
        k, v, en = tok
        for b in writes:
            b.w = tok
            b.r = {}
        for b in reads:
            if b.r.get(k, (0, None))[0] < v:
                b.r[k] = (v, en)

    def op(self, eng, fn, reads=(), writes=(), inc=True):
        e = self.engs[eng]
        self._deps(e, reads, writes)
        ins = fn(e.h)
        if inc:
            e.cnt += 1
            ins.then_inc(e.sem, 1)
            tok = ("E" + eng, e.cnt, eng)
        else:
            tok = ("E" + eng, e.cnt + 1, eng)
        self._mark(tok, reads, writes)
        return ins

    def dma(self, q, out, in_, reads=(), writes=(), tag=None):
        e = self.engs[q]
        self._deps(e, reads, writes)
        key = self._dsem(tag or (writes[0].name if writes else reads[0].name))
        ins = e.h.dma_start(out=out, in_=in_)
        self.dcount[key] += 16
        ins.then_inc(self.sems[key], 16)
        tok = (key, self.dcount[key], "dma")
        self._mark(tok, reads, writes)
        return ins

    def barrier(self):
        for e in self.engs.values():
            for e2 in self.engs.values():
                if e2 is not e and e2.cnt > 0:
                    self._wait(e, "E" + e2.name, e2.cnt)
            for key, cnt in self.dcount.items():
                if cnt > 0:
                    self._wait(e, key, cnt)
        for b in self.bufs:
            b.w = None
            b.r = {}


def build_program(cfg):
    dm = Dims(**cfg)
    D, S, TQ, HALF = dm.D, dm.S, dm.TQ, dm.HALF
    DC, PWC, GC, CWC, NTH, NH, KB, NG, NV = dm.DC, dm.PWC, dm.GC, dm.CWC, dm.NTH, dm.NH, dm.KB, dm.NG, dm.NV
    NBLK = S // 128
    HB = HALF // 128
    groups = dm.groups()

    nc = bass.Bass("TRN2", target_bir_lowering=False)

    def din(name, shape, dt=F32):
        return nc.dram_tensor(name, list(shape), dt, kind="ExternalInput").ap()

    NTT = S // TQ
    x_in = din("x", [NTT, 128, DC, TQ])
    cvec_in = din("cvec", [128, DC])
    w_ada_in = din("w_ada", [L, 3 * DC, 128, DC, 128])
    w_fm_in = din("w_fm", [L, NG, 128, DC, 128])
    w_pool_in = din("w_pool", [L, 4, 128, GC, GC, 128])
    w_a_in = din("w_a", [L, DC, 128, PWC, 128])
    w_c_in = din("w_c", [L, DC, 128, CWC, 128])
    w_b_in = din("w_b", [L, DC, 128, DC, 128])
    w_o_in = din("w_o", [L, DC, 128, DC, 128])
    vecs_in = din("vecs", [128, L, NV])
    fing_in = din("fin_g", [128, DC])
    rcnt_in = din("rcnt", [128, 4, 16])
    cmat_in = din("cmat", [128, 2, 128])
    out_d = nc.dram_tensor("out", [NTT, 128, DC, TQ], F32, kind="ExternalOutput").ap()

    def dscr(name, shape, dt):
        return nc.dram_tensor(name, list(shape), dt).ap()

    qT_s = dscr("qT_s", [DC, 128, S], BF16)
    kT_s = dscr("kT_s", [DC, 128, S], BF16)
    V_s = dscr("V_s", [DC, 128, NBLK * 128], BF16)
    szb_s = dscr("szb_s", [DC, 128, S], BF16)
    g1_s = dscr("g1_s", [DC, 128, S], BF16)
    acc_s = dscr("acc_s", [DC, 128, S], F32)
    x1_s = dscr("x1_s", [NTT, 128, DC, TQ], F32)
    x2_s = x1_s

    XH = 16 + HALF
    N1 = max(DC * HALF, 2 * (2 * S + 2 * HALF), 3 * 2 * DC * 128)
    N2 = max(DC * HALF, 3 * DC * TQ)

    with contextlib.ExitStack() as st:
        sc = Sched(nc, st)

        def sb(name, shape, dt):
            return st.enter_context(nc.sbuf_tensor(name, list(shape), dt))

        def ps(name):
            return st.enter_context(nc.psum_tensor(name, [128, 512], F32))

        R1 = sb("R1", [128, N1], BF16)
        R2 = sb("R2", [128, N2], BF16)
        F0 = sb("F0", [128, XH], F32)
        F1 = sb("F1", [128, XH], F32)
        F2 = sb("F2", [128, XH], F32)
        Bm = sb("Bm", [128, 2 * HALF], BF16)
        STG = [sb("STG%d" % i, [128, HALF], BF16) for i in range(2)]
        TMPB = [sb("TMPB%d" % i, [128, TQ], BF16) for i in range(2)]
        TMPF = [sb("TMPF%d" % i, [128, TQ], F32) for i in range(2)]
        WS = [sb("WS%d" % i, [128, DC, 128], BF16) for i in range(3)]
        WP = sb("WP", [128, GC, GC, 128], BF16)
        VEC = sb("VEC", [128, L, NV], F32)
        MOD = sb("MOD", [128, L, 3 * DC], F32)
        GS = sb("GS", [128, L, DC], F32)
        FING = sb("FING", [128, DC], F32)
        SCV = sb("SCV", [128, DC], F32)
        RCNT = sb("RCNT", [128, 4, 16], F32)
        NEGL = sb("NEGL", [128, 128], BF16)
        MASK = sb("MASK", [128, 128], F32)
        ONES = sb("ONES", [128, 128], BF16)
        NEGONES = sb("NEGONES", [128, 128], BF16)
        HXA = sb("HXA", [128, PWC, 16], F32)
        HV = sb("HV", [128, CWC, 2], F32)
        RSTD = sb("RSTD", [128, TQ], F32)
        T16 = sb("T16", [128, 16], F32)
        PS = [ps("PS%d" % i) for i in range(8)]

        b_R1 = sc.buf("R1")
        b_R2 = sc.buf("R2")
        b_F0, b_F1, b_F2 = sc.buf("F0"), sc.buf("F1"), sc.buf("F2")
        b_Bm = sc.buf("Bm")
        b_STG = [sc.buf("STG0"), sc.buf("STG1")]
        b_TMPB = [sc.buf("TMPB0"), sc.buf("TMPB1")]
        b_TMPF = [sc.buf("TMPF0"), sc.buf("TMPF1")]
        b_WS = [sc.buf("WS%d" % i) for i in range(3)]
        b_WP = sc.buf("WP")
        b_small = sc.buf("small")
        b_HXA, b_HV = sc.buf("HXA"), sc.buf("HV")
        b_RSTD = sc.buf("RSTD")
        b_T16 = sc.buf("T16")
        b_PS = [sc.buf("PS%d" % i) for i in range(8)]
        b_qT, b_kT, b_V, b_szb = sc.buf("qT"), sc.buf("kT"), sc.buf("V"), sc.buf("szb")
        b_g1, b_acc = sc.buf("g1"), sc.buf("acc")
        b_x1, b_x2, b_out = sc.buf("x1"), sc.buf("x2"), sc.buf("outd")

        ws_ctr = [0]

        def next_ws():
            i = ws_ctr[0] % 3
            ws_ctr[0] += 1
            return i

        psA_ctr = [0]

        def next_psA():
            i = psA_ctr[0] % 4
            psA_ctr[0] += 1
            return i

        sc.dma("sp", VEC[:], vecs_in, writes=[b_small], tag="small")
        sc.dma("sp", FING[:], fing_in, writes=[b_small], tag="small")
        sc.dma("sp", SCV[:], cvec_in, writes=[b_small], tag="small")
        sc.dma("sp", RCNT[:], rcnt_in, writes=[b_small], tag="small")
        sc.dma("sp", MASK[:], cmat_in[:, 1, :], writes=[b_small], tag="small")
        sc.dma("pool", NEGL[:], cmat_in[:, 0, :], writes=[b_small], tag="small")
        sc.op("dve", lambda e: e.memset(ONES[:], 1.0), writes=[b_small])
        sc.op("dve", lambda e: e.memset(NEGONES[:], -1.0), writes=[b_small])
        sc.op("act", lambda e: e.activation(out=SCV[:], in_=SCV[:], func=AF.Silu),
              reads=[b_small], writes=[b_small])

        WA = [R1[:, i * 2 * DC * 128:(i + 1) * 2 * DC * 128].bitcast(F32).rearrange(
            "p (c j) -> p c j", c=DC) for i in range(3)]
        b_WA = [sc.buf("WA%d" % i) for i in range(3)]
        for l in range(L):
            for g in range(3 * DC):
                i = (l * 3 * DC + g) % 3
                sc.dma("sp", WA[i], w_ada_in[l, g], writes=[b_WA[i]], tag="wa%d" % i)
                for kc in range(DC):
                    sc.op("pe", lambda e, i=i, kc=kc, g=g: e.matmul(
                        PS[0][:, g:g + 1], lhsT=WA[i][:, kc, :], rhs=SCV[:, kc:kc + 1],
                        start=(kc == 0), stop=(kc == DC - 1)),
                        reads=[b_WA[i], b_small], writes=[b_PS[0]], inc=(kc == DC - 1))
            sc.op("dve", lambda e, l=l: e.tensor_tensor(
                out=MOD[:, l, :], in0=PS[0][:, 0:3 * DC],
                in1=VEC[:, l, dm.o_bada:dm.o_bada + 3 * DC], op=ALU.add),
                reads=[b_PS[0], b_small], writes=[b_small])
            sc.op("dve", lambda e, l=l: e.scalar_tensor_tensor(
                out=GS[:, l, :], in0=MOD[:, l, DC:2 * DC], scalar=1.0,
                in1=VEC[:, l, dm.o_ng:dm.o_ng + DC], op0=ALU.add, op1=ALU.mult),
                reads=[b_small], writes=[b_small])
        sc.barrier()

        hT = R1[:, 0:DC * HALF].rearrange("p (c t) -> p c t", c=DC)
        merged = hT
        xt = R2[:, 0:2 * DC * TQ].bitcast(F32).rearrange("p (c t) -> p c t", c=DC)
        sq = R2[:, 2 * DC * TQ:3 * DC * TQ].rearrange("p (c t) -> p c t", c=DC)
        ya = R2[:, 0:PWC * HALF].rearrange("p (c t) -> p c t", c=PWC)
        yc = R2[:, PWC * HALF:DC * HALF].rearrange("p (c t) -> p c t", c=CWC)
        yb = R2[:, 0:DC * HALF].rearrange("p (c t) -> p c t", c=DC)
        mixed = Bm[:, :].rearrange("p (c t) -> p c t", c=2)
        CGt = Bm[:, 0:HALF]
        SG0 = Bm[:, 0:HALF]
        SG2 = Bm[:, HALF:2 * HALF]

        def norm_tile(src, t0, gvec, shift, dst_fn, final_dst=None):
            sc.dma("sp", xt, src[t0 // TQ],
                   reads=[], writes=[b_R2], tag="xt")
            n4 = max(1, DC // 4)
            for c0 in range(0, DC, n4):
                sc.op("act", lambda e, c0=c0: e.activation(
                    out=sq[:, c0:c0 + n4, :], in_=xt[:, c0:c0 + n4, :], func=AF.Square),
                    reads=[b_R2], writes=[b_R2])
            for c in range(DC):
                sc.op("pe", lambda e, c=c: e.matmul(
                    PS[4][:, 0:TQ], lhsT=ONES[:], rhs=sq[:, c, :],
                    start=(c == 0), stop=(c == DC - 1)),
                    reads=[b_R2, b_small], writes=[b_PS[4]], inc=(c == DC - 1))
            sc.op("dve", lambda e: e.tensor_scalar(
                out=RSTD[:], in0=PS[4][:, 0:TQ], scalar1=1.0 / D, scalar2=RMS_EPS,
                op0=ALU.mult, op1=ALU.add), reads=[b_PS[4]], writes=[b_RSTD])
            sc.op("act", lambda e: e.activation(out=RSTD[:], in_=RSTD[:], func=AF.Ln),
                  reads=[b_RSTD], writes=[b_RSTD])
            sc.op("act", lambda e: e.activation(out=RSTD[:], in_=RSTD[:], func=AF.Exp, scale=-0.5),
                  reads=[b_RSTD], writes=[b_RSTD])
            for c in range(DC):
                sc.op("dve", lambda e, c=c: e.scalar_tensor_tensor(
                    out=xt[:, c, :], in0=xt[:, c, :], scalar=gvec[:, c:c + 1], in1=RSTD[:],
                    op0=ALU.mult, op1=ALU.mult), reads=[b_R2, b_RSTD, b_small], writes=[b_R2])
                if dst_fn is not None:
                    sc.op("act", lambda e, c=c: e.activation(
                        out=dst_fn(c), in_=xt[:, c, :], func=AF.Identity,
                        bias=shift[:, c:c + 1], scale=1.0),
                        reads=[b_R2, b_small], writes=[b_R1])
            if final_dst is not None:
                sc.dma("sp", final_dst[t0 // TQ], xt,
                       reads=[b_R2], writes=[b_out], tag="outd")

        def load_w(src_ap, nk):
            i = next_ws()
            sc.dma("pool", WS[i][:, 0:nk, :], src_ap, writes=[b_WS[i]], tag="ws%d" % i)
            return i

        def mm_tile(wi, nk, rhs_fn, rhs_bufs, j):
            pi = next_psA()
            for kc in range(nk):
                sc.op("pe", lambda e, kc=kc: e.matmul(
                    PS[pi][:, 0:TQ], lhsT=WS[wi][:, kc, :], rhs=rhs_fn(kc, j),
                    start=(kc == 0), stop=(kc == nk - 1)),
                    reads=[b_WS[wi]] + rhs_bufs, writes=[b_PS[pi]], inc=(kc == nk - 1))
            return pi

        def tsl(j):
            return slice(j * TQ, (j + 1) * TQ)

        stg_ctr = [0]
        tmp_ctr = [0]

        for l in range(L):
            xin = x_in if l == 0 else x1_s
            b_xin = None if l == 0 else b_x1
            xout = x1_s if l == 0 else x2_s
            b_xout = b_x1 if l == 0 else b_x2
            shiftv = MOD[:, l, 0:DC]
            rgv = MOD[:, l, 2 * DC:3 * DC]
            gsv = GS[:, l, :]
            psv = VEC[:, l, dm.o_ps:dm.o_ps + PWC]
            cwv = VEC[:, l, dm.o_cw:dm.o_cw + 3 * CWC]
            bgv = VEC[:, l, dm.o_bg:dm.o_bg + 3 * DC]

            for h in range(NH):
                T0 = h * HALF
                hs = slice(T0, T0 + HALF)
                for j in range(NTH):
                    norm_tile(xin, T0 + j * TQ, gsv, shiftv,
                              lambda c, j=j: hT[:, c, tsl(j)])
                sc.barrier()

                hrhs = lambda kc, j: hT[:, kc, tsl(j)]
                for (kind, idx, src, col) in groups:
                    gidx = groups.index((kind, idx, src, col))
                    wi = load_w(w_fm_in[l, gidx], DC)
                    if kind == "xa":
                        g = idx // GC
                        gi = idx % GC
                        w = POOL_WINDOWS[g]
                        if h == 0:
                            sc.op("dve", lambda e: e.memset(F0[:, 0:16], 0.0), writes=[b_F0])
                        else:
                            sc.op("dve", lambda e, idx=idx: e.tensor_copy(out=F0[:, 0:16], in_=HXA[:, idx, :]),
                                  reads=[b_HXA], writes=[b_F0])
                        for j in range(NTH):
                            pi = mm_tile(wi, DC, hrhs, [b_R1], j)
                            sc.op("act", lambda e, pi=pi, j=j: e.activation(
                                out=F0[:, 16 + j * TQ:16 + (j + 1) * TQ], in_=PS[pi][:, 0:TQ], func=AF.Copy),
                                reads=[b_PS[pi]], writes=[b_F0])
                        sc.op("dve", lambda e, idx=idx: e.tensor_copy(out=HXA[:, idx, :], in_=F0[:, HALF:HALF + 16]),
                              reads=[b_F0], writes=[b_HXA])
                        cur, cb = F0, b_F0
                        step = 1
                        k = 0
                        while step < w:
                            nxt, nb = (F1, b_F1) if k % 2 == 0 else (F2, b_F2)
                            sc.op("dve", lambda e, cur=cur, nxt=nxt, step=step: e.tensor_tensor(
                                out=nxt[:, step:XH], in0=cur[:, step:XH], in1=cur[:, 0:XH - step], op=ALU.add),
                                reads=[cb], writes=[nb])
                            cur, cb = nxt, nb
                            step *= 2
                            k += 1
                        sc.op("dve", lambda e, cur=cur, gi=gi, w=w: e.scalar_tensor_tensor(
                            out=mixed[:, gi, :], in0=cur[:, 16:XH], scalar=1.0 / w, in1=F0[:, 16:XH],
                            op0=ALU.mult, op1=ALU.subtract), reads=[cb, b_F0], writes=[b_Bm])
                        if h == 0:
                            sc.op("dve", lambda e, cur=cur, g=g: e.tensor_tensor(
                                out=T16[:], in0=cur[:, 16:32], in1=RCNT[:, g, :], op=ALU.mult),
                                reads=[cb, b_small], writes=[b_T16])
                            sc.op("dve", lambda e, gi=gi: e.tensor_tensor(
                                out=mixed[:, gi, 0:16], in0=T16[:], in1=F0[:, 16:32], op=ALU.subtract),
                                reads=[b_T16, b_F0], writes=[b_Bm])
                        if gi == GC - 1:
                            sc.dma("pool", WP[:], w_pool_in[l, g], writes=[b_WP], tag="wp")
                            for oc in range(GC):
                                ch = g * GC + oc
                                for j in range(NTH):
                                    pi = next_psA()
                                    for kc in range(GC):
                                        sc.op("pe", lambda e, oc=oc, kc=kc, j=j, pi=pi: e.matmul(
                                            PS[pi][:, 0:TQ], lhsT=WP[:, oc, kc, :], rhs=mixed[:, kc, tsl(j)],
                                            start=(kc == 0), stop=(kc == GC - 1)),
                                            reads=[b_WP, b_Bm], writes=[b_PS[pi]], inc=(kc == GC - 1))
                                    sc.op("dve", lambda e, pi=pi, ch=ch, j=j: e.tensor_scalar(
                                        out=ya[:, ch, tsl(j)], in0=PS[pi][:, 0:TQ], scalar1=psv[:, ch:ch + 1],
                                        scalar2=None, op0=ALU.mult), reads=[b_PS[pi], b_small], writes=[b_R2])
                    elif kind == "za":
                        for j in range(NTH):
                            pi = mm_tile(wi, DC, hrhs, [b_R1], j)
                            ti = tmp_ctr[0] % 2
                            tmp_ctr[0] += 1
                            sc.op("act", lambda e, pi=pi, ti=ti: e.activation(
                                out=TMPB[ti][:], in_=PS[pi][:, 0:TQ], func=AF.Silu),
                                reads=[b_PS[pi]], writes=[b_TMPB[ti]])
                            sc.op("dve", lambda e, ti=ti, idx=idx, j=j: e.tensor_tensor(
                                out=ya[:, idx, tsl(j)], in0=ya[:, idx, tsl(j)], in1=TMPB[ti][:], op=ALU.mult),
                                reads=[b_TMPB[ti], b_R2], writes=[b_R2])
                    elif kind == "cg":
                        for j in range(NTH):
                            pi = mm_tile(wi, DC, hrhs, [b_R1], j)
                            sc.op("act", lambda e, pi=pi, j=j: e.activation(
                                out=CGt[:, tsl(j)], in_=PS[pi][:, 0:TQ], func=AF.Copy),
                                reads=[b_PS[pi]], writes=[b_Bm])
                    elif kind == "u":
                        if h == 0:
                            sc.op("dve", lambda e: e.memset(F0[:, 0:2], 0.0), writes=[b_F0])
                        else:
                            sc.op("dve", lambda e, idx=idx: e.tensor_copy(out=F0[:, 0:2], in_=HV[:, idx, :]),
                                  reads=[b_HV], writes=[b_F0])
                        for j in range(NTH):
                            pi = mm_tile(wi, DC, hrhs, [b_R1], j)
                            sc.op("dve", lambda e, pi=pi, j=j: e.tensor_tensor(
                                out=F0[:, 2 + j * TQ:2 + (j + 1) * TQ], in0=PS[pi][:, 0:TQ], in1=CGt[:, tsl(j)],
                                op=ALU.mult), reads=[b_PS[pi], b_Bm], writes=[b_F0])
                        sc.op("dve", lambda e, idx=idx: e.tensor_copy(out=HV[:, idx, :], in_=F0[:, HALF:HALF + 2]),
                              reads=[b_F0], writes=[b_HV])
                        sc.op("dve", lambda e, idx=idx: e.tensor_scalar(
                            out=F1[:, 0:HALF], in0=F0[:, 2:2 + HALF], scalar1=cwv[:, 2 * CWC + idx:2 * CWC + idx + 1],
                            scalar2=None, op0=ALU.mult), reads=[b_F0, b_small], writes=[b_F1])
                        sc.op("dve", lambda e, idx=idx: e.scalar_tensor_tensor(
                            out=F1[:, 0:HALF], in0=F0[:, 1:1 + HALF], scalar=cwv[:, CWC + idx:CWC + idx + 1],
                            in1=F1[:, 0:HALF], op0=ALU.mult, op1=ALU.add), reads=[b_F0, b_F1, b_small], writes=[b_F1])
                        sc.op("dve", lambda e, idx=idx: e.scalar_tensor_tensor(
                            out=F1[:, 0:HALF], in0=F0[:, 0:HALF], scalar=cwv[:, idx:idx + 1],
                            in1=F1[:, 0:HALF], op0=ALU.mult, op1=ALU.add), reads=[b_F0, b_F1, b_small], writes=[b_F1])
                    elif kind == "bg":
                        for j in range(NTH):
                            pi = mm_tile(wi, DC, hrhs, [b_R1], j)
                            sc.op("dve", lambda e, pi=pi, j=j: e.tensor_tensor(
                                out=F1[:, tsl(j)], in0=PS[pi][:, 0:TQ], in1=F1[:, tsl(j)], op=ALU.mult),
                                reads=[b_PS[pi], b_F1], writes=[b_F1])
                    elif kind == "zc":
                        for j in range(NTH):
                            pi = mm_tile(wi, DC, hrhs, [b_R1], j)
                            ti = tmp_ctr[0] % 2
                            tmp_ctr[0] += 1
                            sc.op("act", lambda e, pi=pi, ti=ti: e.activation(
                                out=TMPF[ti][:], in_=PS[pi][:, 0:TQ], func=AF.Silu),
                                reads=[b_PS[pi]], writes=[b_TMPF[ti]])
                            sc.op("dve", lambda e, ti=ti, idx=idx, j=j: e.tensor_tensor(
                                out=yc[:, idx, tsl(j)], in0=TMPF[ti][:], in1=F1[:, tsl(j)], op=ALU.mult),
                                reads=[b_TMPF[ti], b_F1], writes=[b_R2])
                    elif kind in ("q", "k", "zb", "g1"):
                        si = stg_ctr[0] % 2
                        stg_ctr[0] += 1
                        for j in range(NTH):
                            pi = mm_tile(wi, DC, hrhs, [b_R1], j)
                            if kind == "q":
                                fn = lambda e, pi=pi, j=j, si=si: e.activation(
                                    out=STG[si][:, tsl(j)], in_=PS[pi][:, 0:TQ], func=AF.Copy, scale=float(128 ** -0.5))
                                rd = [b_PS[pi]]
                            elif kind == "k":
                                fn = lambda e, pi=pi, j=j, si=si: e.activation(
                                    out=STG[si][:, tsl(j)], in_=PS[pi][:, 0:TQ], func=AF.Copy)
                                rd = [b_PS[pi]]
                            elif kind == "zb":
                                fn = lambda e, pi=pi, j=j, si=si: e.activation(
                                    out=STG[si][:, tsl(j)], in_=PS[pi][:, 0:TQ], func=AF.Silu)
                                rd = [b_PS[pi]]
                            else:
                                fn = lambda e, pi=pi, j=j, si=si, idx=idx: e.activation(
                                    out=STG[si][:, tsl(j)], in_=PS[pi][:, 0:TQ], func=AF.Sigmoid,
                                    bias=bgv[:, DC + idx:DC + idx + 1], scale=1.0)
                                rd = [b_PS[pi], b_small]
                            sc.op("act", fn, reads=rd, writes=[b_STG[si]])
                        dst, db = {"q": (qT_s, b_qT), "k": (kT_s, b_kT), "zb": (szb_s, b_szb),
                                   "g1": (g1_s, b_g1)}[kind]
                        sc.dma("sp", dst[idx, :, hs], STG[si][:], reads=[b_STG[si]], writes=[db],
                               tag="st_" + kind)
                    elif kind == "v":
                        si = stg_ctr[0] % 2
                        stg_ctr[0] += 1
                        nb4 = min(4, HB)
                        for tb0 in range(0, HB, nb4):
                            pi = next_psA()
                            for bi in range(nb4):
                                tb = tb0 + bi
                                for kc in range(DC):
                                    sc.op("pe", lambda e, kc=kc, tb=tb, bi=bi, pi=pi: e.matmul(
                                        PS[pi][:, bi * 128:(bi + 1) * 128], lhsT=hT[:, kc, tb * 128:(tb + 1) * 128],
                                        rhs=WS[wi][:, kc, :], start=(kc == 0), stop=(kc == DC - 1)),
                                        reads=[b_WS[wi], b_R1], writes=[b_PS[pi]], inc=(kc == DC - 1))
                            sc.op("act", lambda e, pi=pi, tb0=tb0, si=si: e.activation(
                                out=STG[si][:, tb0 * 128:(tb0 + nb4) * 128], in_=PS[pi][:, 0:nb4 * 128], func=AF.Copy),
                                reads=[b_PS[pi]], writes=[b_STG[si]])
                        sc.dma("sp", V_s[idx, :, h * HALF:(h + 1) * HALF], STG[si][:], reads=[b_STG[si]],
                               writes=[b_V], tag="st_v")
                    elif kind in ("g0", "g2"):
                        dstt = SG0 if kind == "g0" else SG2
                        boff = 0 if kind == "g0" else 2 * DC
                        for j in range(NTH):
                            pi = mm_tile(wi, DC, hrhs, [b_R1], j)
                            sc.op("act", lambda e, pi=pi, j=j, dstt=dstt, boff=boff, idx=idx: e.activation(
                                out=dstt[:, tsl(j)], in_=PS[pi][:, 0:TQ], func=AF.Sigmoid,
                                bias=bgv[:, boff + idx:boff + idx + 1], scale=1.0),
                                reads=[b_PS[pi], b_small], writes=[b_Bm])
                    if kind == "g1":
                        c = idx
                        AC, b_AC = (F1, b_F1) if c % 2 == 0 else (F2, b_F2)
                        wa = load_w(w_a_in[l, c], PWC)
                        for j in range(NTH):
                            pi = mm_tile(wa, PWC, lambda kc, j: ya[:, kc, tsl(j)], [b_R2], j)
                            sc.op("dve", lambda e, pi=pi, j=j, AC=AC: e.tensor_tensor(
                                out=AC[:, tsl(j)], in0=PS[pi][:, 0:TQ], in1=SG0[:, tsl(j)], op=ALU.mult),
                                reads=[b_PS[pi], b_Bm], writes=[b_AC])
                        wc = load_w(w_c_in[l, c], CWC)
                        for j in range(NTH):
                            pi = mm_tile(wc, CWC, lambda kc, j: yc[:, kc, tsl(j)], [b_R2], j)
                            ti = tmp_ctr[0] % 2
                            tmp_ctr[0] += 1
                            sc.op("dve", lambda e, pi=pi, j=j, ti=ti: e.tensor_tensor(
                                out=TMPF[ti][:], in0=PS[pi][:, 0:TQ], in1=SG2[:, tsl(j)], op=ALU.mult),
                                reads=[b_PS[pi], b_Bm], writes=[b_TMPF[ti]])
                            sc.op("dve", lambda e, j=j, ti=ti, AC=AC: e.tensor_tensor(
                                out=AC[:, tsl(j)], in0=AC[:, tsl(j)], in1=TMPF[ti][:], op=ALU.add),
                                reads=[b_TMPF[ti], b_AC], writes=[b_AC])
                        sc.dma("sp", acc_s[c, :, hs], AC[:, 0:HALF], reads=[b_AC], writes=[b_acc], tag="st_acc")
                sc.barrier()

                nkeys = (h + 1) * HALF
                nkb = nkeys // 128
                per = 2 * S + 2 * HALF
                KT = [R1[:, i * per:i * per + S] for i in range(2)]
                VS = [R1[:, i * per + S:i * per + 2 * S].rearrange("p (b d) -> p b d", d=128) for i in range(2)]
                QS = [R1[:, i * per + 2 * S:i * per + 2 * S + HALF] for i in range(2)]
                ZS = [R1[:, i * per + 2 * S + HALF:i * per + 2 * S + 2 * HALF] for i in range(2)]
                b_KT = [sc.buf("KT%d" % i) for i in range(2)]
                b_VS = [sc.buf("VS%d" % i) for i in range(2)]
                b_QS = [sc.buf("QS%d" % i) for i in range(2)]
                b_ZS = [sc.buf("ZS%d" % i) for i in range(2)]
                SB = 16
                spool = []
                for Fx in (F0, F1, F2):
                    fb = Fx[:, 0:(XH // 2) * 2].bitcast(BF16)
                    for i in range((XH // 2) * 2 * 2 // TQ):
                        spool.append(fb[:, i * TQ:(i + 1) * TQ])
                for i in range(2 * HALF // TQ):
                    spool.append(Bm[:, i * TQ:(i + 1) * TQ])
                spool = spool[:2 * SB]
                NSP = len(spool)
                b_SP = [sc.buf("SP%d" % i) for i in range(NSP)]
                Ab = [STG[0][:, 0:TQ], STG[1][:, 0:TQ], TMPB[1][:, 0:TQ]]
                b_A = [sc.buf("A%d" % i) for i in range(3)]
                RS = TMPB[0][:, 0:TQ]
                b_RS = sc.buf("RS")

                items = []
                for hh in range(DC):
                    for j in range(NTH):
                        qi = h * NTH + j
                        blocks = []
                        for kt in range(qi, -1, -1):
                            for kb in range(KB - 1, -1, -1):
                                blocks.append((kt * KB + kb, kt == qi, kb * 128 if kt == qi else 0))
                        nb = len(blocks)
                        bsz = min(SB, NSP // 2)
                        for b0 in range(0, nb, bsz):
                            items.append(dict(hh=hh, j=j, blocks=blocks, b0=b0, b1=min(nb, b0 + bsz), nb=nb,
                                              unit=hh * NTH + j))
                loaded = set()
                sp_ctr = [0]
                a_ctr = [0]
                pz_ctr = [0]
                pc_ctr = [0]

                def ensure_loaded(hh):
                    if hh in loaded:
                        return
                    loaded.add(hh)
                    bs = hh % 2
                    sc.dma("sp", KT[bs][:, 0:nkeys], kT_s[hh, :, 0:nkeys], writes=[b_KT[bs]], tag="kt%d" % bs)
                    sc.dma("sp", VS[bs][:, 0:nkb, :], V_s[hh, :, 0:nkeys].rearrange("p (b d) -> p b d", d=128),
                           writes=[b_VS[bs]], tag="vs%d" % bs)
                    sc.dma("sp", QS[bs], qT_s[hh, :, hs], writes=[b_QS[bs]], tag="qs%d" % bs)
                    sc.dma("sp", ZS[bs], szb_s[hh, :, hs], writes=[b_ZS[bs]], tag="zs%d" % bs)

                def S1(it):
                    hh, j, bs = it["hh"], it["j"], it["hh"] % 2
                    ensure_loaded(hh)
                    it["slots"] = []
                    for i in range(it["b0"], it["b1"]):
                        gb, diag, c0 = it["blocks"][i]
                        slot = sp_ctr[0] % NSP
                        sp_ctr[0] += 1
                        it["slots"].append(slot)
                        pz = (0, 1, 6, 7)[pz_ctr[0] % 4]
                        pz_ctr[0] += 1
                        qsl = QS[bs][:, j * TQ + c0:(j + 1) * TQ]
                        sc.op("pe", lambda e: e.matmul(PS[pz][:, c0:TQ], lhsT=KT[bs][:, gb * 128:(gb + 1) * 128],
                                                       rhs=qsl, start=True, stop=True),
                              reads=[b_KT[bs], b_QS[bs]], writes=[b_PS[pz]])
                        sc.op("act", lambda e: e.activation(out=spool[slot][:, c0:TQ], in_=PS[pz][:, c0:TQ],
                                                            func=AF.Softplus),
                              reads=[b_PS[pz]], writes=[b_SP[slot]])
                        if diag:
                            sc.op("dve", lambda e: e.tensor_tensor(
                                out=spool[slot][:, c0:c0 + 128], in0=spool[slot][:, c0:c0 + 128], in1=MASK[:],
                                op=ALU.mult), reads=[b_SP[slot], b_small], writes=[b_SP[slot]])

                def S2(it):
                    hh, j, bs, nb = it["hh"], it["j"], it["hh"] % 2, it["nb"]
                    pyi = 4 + it["unit"] % 2
                    if it["b0"] == 0:
                        sc.op("dve", lambda e: e.memset(RS, 0.0), writes=[b_RS])
                    pend = None

                    def av(i, ai):
                        gb, diag, c0 = it["blocks"][i]
                        sc.op("pe", lambda e: e.matmul(PS[pyi][:, c0:TQ], lhsT=VS[bs][:, gb, :], rhs=Ab[ai][:, c0:TQ],
                                                       start=(i == 0), stop=(i == nb - 1), skip_group_check=True),
                              reads=[b_VS[bs], b_A[ai]], writes=[b_PS[pyi]], inc=(i == nb - 1))

                    for n, i in enumerate(range(it["b0"], it["b1"])):
                        gb, diag, c0 = it["blocks"][i]
                        slot = it["slots"][n]
                        pc = 2 + pc_ctr[0] % 2
                        pc_ctr[0] += 1
                        ai = a_ctr[0] % 3
                        a_ctr[0] += 1
                        qsl = QS[bs][:, j * TQ + c0:(j + 1) * TQ]
                        first = (i == 0)
                        sc.op("pe", lambda e: e.matmul(PS[pc][:, c0:TQ], lhsT=KT[bs][:, gb * 128:(gb + 1) * 128],
                                                       rhs=qsl, start=True, stop=False),
                              reads=[b_KT[bs], b_QS[bs]], writes=[b_PS[pc]], inc=False)
                        sc.op("pe", lambda e: e.matmul(PS[pc][:, c0:TQ], lhsT=NEGL[:], rhs=spool[slot][:, c0:TQ],
                                                       start=False, stop=first),
                              reads=[b_SP[slot], b_small], writes=[b_PS[pc]], inc=first)
                        if not first:
                            sc.op("pe", lambda e: e.matmul(PS[pc][:, c0:TQ], lhsT=NEGONES[:], rhs=RS[:, c0:TQ],
                                                           start=False, stop=True),
                                  reads=[b_RS, b_small], writes=[b_PS[pc]], inc=True)
                        if i < nb - 1:
                            sc.op("dve", lambda e: e.tensor_tensor(out=RS[:, c0:TQ], in0=RS[:, c0:TQ],
                                                                   in1=spool[slot][:, c0:TQ], op=ALU.add),
                                  reads=[b_SP[slot], b_RS], writes=[b_RS])
                        sc.op("act", lambda e: e.activation(out=Ab[ai][:, c0:TQ], in_=PS[pc][:, c0:TQ], func=AF.Exp),
                              reads=[b_PS[pc]], writes=[b_A[ai]])
                        if diag:
                            sc.op("dve", lambda e: e.tensor_tensor(
                                out=Ab[ai][:, c0:c0 + 128], in0=Ab[ai][:, c0:c0 + 128], in1=MASK[:], op=ALU.mult),
                                reads=[b_A[ai], b_small], writes=[b_A[ai]])
                        if pend is not None:
                            av(*pend)
                        pend = (i, ai)
                    av(*pend)
                    if it["b1"] == nb:
                        sc.op("dve", lambda e: e.tensor_tensor(
                            out=yb[:, hh, tsl(j)], in0=PS[pyi][:, 0:TQ], in1=ZS[bs][:, tsl(j)], op=ALU.mult),
                            reads=[b_PS[pyi], b_ZS[bs]], writes=[b_R2])

                S1(items[0])
                for n in range(len(items)):
                    if n + 1 < len(items):
                        S1(items[n + 1])
                    S2(items[n])
                sc.barrier()

                G1b = [STG[0], STG[1]]
                b_G1 = [sc.buf("G1_0"), sc.buf("G1_1")]
                ACb = [F1, F2]
                b_ACb = [sc.buf("ACb0"), sc.buf("ACb1")]
                for c in range(DC):
                    bs = c % 2
                    wb = load_w(w_b_in[l, c], DC)
                    sc.dma("sp", G1b[bs][:], g1_s[c, :, hs], writes=[b_G1[bs]], tag="g1l%d" % bs)
                    sc.dma("sp", ACb[bs][:, 0:HALF], acc_s[c, :, hs], writes=[b_ACb[bs]], tag="acl%d" % bs)
                    for j in range(NTH):
                        pi = mm_tile(wb, DC, lambda kc, j: yb[:, kc, tsl(j)], [b_R2], j)
                        ti = tmp_ctr[0] % 2
                        tmp_ctr[0] += 1
                        sc.op("dve", lambda e, pi=pi, j=j, ti=ti, bs=bs: e.tensor_tensor(
                            out=TMPF[ti][:], in0=PS[pi][:, 0:TQ], in1=G1b[bs][:, tsl(j)], op=ALU.mult),
                            reads=[b_PS[pi], b_G1[bs]], writes=[b_TMPF[ti]])
                        sc.op("dve", lambda e, j=j, ti=ti, bs=bs, c=c: e.tensor_tensor(
                            out=merged[:, c, tsl(j)], in0=TMPF[ti][:], in1=ACb[bs][:, tsl(j)], op=ALU.add),
                            reads=[b_TMPF[ti], b_ACb[bs]], writes=[b_R1])
                XRb = [F1, F2]
                for c in range(DC):
                    bs = c % 2
                    wo = load_w(w_o_in[l, c], DC)
                    sc.dma("sp", XRb[bs][:, 0:HALF].rearrange("p (n t) -> p n t", t=TQ),
                           xin[h * NTH:(h + 1) * NTH, :, c, :].rearrange("n p t -> p n t"),
                           writes=[b_ACb[bs]], tag="xrl%d" % bs)
                    for j in range(NTH):
                        pi = mm_tile(wo, DC, lambda kc, j: merged[:, kc, tsl(j)], [b_R1], j)
                        sc.op("dve", lambda e, pi=pi, j=j, bs=bs, c=c: e.scalar_tensor_tensor(
                            out=XRb[bs][:, tsl(j)], in0=PS[pi][:, 0:TQ], scalar=rgv[:, c:c + 1],
                            in1=XRb[bs][:, tsl(j)], op0=ALU.mult, op1=ALU.add),
                            reads=[b_PS[pi], b_ACb[bs], b_small], writes=[b_ACb[bs]])
                    sc.dma("sp", xout[h * NTH:(h + 1) * NTH, :, c, :].rearrange("n p t -> p n t"),
                           XRb[bs][:, 0:HALF].rearrange("p (n t) -> p n t", t=TQ), reads=[b_ACb[bs]],
                           writes=[b_xout], tag="st_x")
                sc.barrier()

        for t in range(S // TQ):
            norm_tile(x2_s, t * TQ, FING, None, None, final_dst=out_d)
        sc.barrier()
    return nc


def _fm_weight(w, K):
    n = w.shape[1] // 128
    return np.ascontiguousarray(w.reshape(K // 128, 128, n, 128).transpose(2, 1, 0, 3))


def _fm_vec(v):
    return np.ascontiguousarray(v.reshape(-1, 128).T)


def prep_inputs(inp, cfg):
    dm = Dims(**cfg)
    D, S, DC = dm.D, dm.S, dm.DC
    groups = dm.groups()
    B = inp["x"].shape[0]
    f32 = np.float32
    w_ada = np.stack([_fm_weight(np.asarray(inp["w_ada"][l], f32), D) for l in range(L)])
    w_fm = []
    for l in range(L):
        wi = np.asarray(inp["w_in"][l], f32)
        wg = np.asarray(inp["w_gate"][l], f32)
        cols = [(wi if src == "in" else wg)[:, col:col + 128] for (_, _, src, col) in groups]
        w_fm.append(_fm_weight(np.concatenate(cols, axis=1), D))
    w_fm = np.stack(w_fm)
    GCd = dm.GC * 128
    pw = np.asarray(inp["pool_w"], f32).reshape(L, 4, dm.GC, 128, dm.GC, 128).transpose(0, 1, 3, 4, 2, 5)
    w_pool = np.ascontiguousarray(pw)
    w_a = np.stack([_fm_weight(np.asarray(inp["w_br_a"][l], f32), dm.PW) for l in range(L)])
    w_c = np.stack([_fm_weight(np.asarray(inp["w_br_c"][l], f32), dm.CW) for l in range(L)])
    w_b = np.stack([_fm_weight(np.asarray(inp["w_br_b"][l], f32), D) for l in range(L)])
    w_o = np.stack([_fm_weight(np.asarray(inp["w_out"][l], f32), D) for l in range(L)])
    vecs = np.zeros((128, L, dm.NV), f32)
    for l in range(L):
        vecs[:, l, dm.o_ng:dm.o_ng + DC] = _fm_vec(np.asarray(inp["norm_g"][l], f32))
        vecs[:, l, dm.o_bada:dm.o_bada + 3 * DC] = _fm_vec(np.asarray(inp["b_ada"][l], f32))
        vecs[:, l, dm.o_ps:dm.o_ps + dm.PWC] = _fm_vec(np.asarray(inp["pool_scale"][l], f32))
        cw = np.asarray(inp["conv_w"][l], f32)
        vecs[:, l, dm.o_cw:dm.o_cw + 3 * dm.CWC] = np.concatenate([_fm_vec(cw[i]) for i in range(3)], axis=1)
        vecs[:, l, dm.o_bg:dm.o_bg + 3 * DC] = _fm_vec(np.asarray(inp["b_gate"][l], f32))
    fin_g = _fm_vec(np.asarray(inp["final_g"], f32))
    rc = np.zeros((128, 4, 16), f32)
    for g, w in enumerate(POOL_WINDOWS):
        rc[:, g, :] = 1.0 / np.minimum(np.arange(16) + 1, w).astype(f32)
    cm = np.zeros((128, 2, 128), f32)
    p = np.arange(128)[:, None]
    jj = np.arange(128)[None, :]
    cm[:, 0, :] = np.where(p >= jj, -1.0, 0.0)
    cm[:, 1, :] = np.where(p < jj, 1.0, 0.0)
    shared = dict(w_ada=w_ada, w_fm=w_fm, w_pool=w_pool, w_a=w_a, w_c=w_c, w_b=w_b, w_o=w_o,
                  vecs=vecs, fin_g=fin_g, rcnt=rc, cmat=cm)
    idle = {k: np.zeros_like(v) for k, v in shared.items()}
    in_maps = []
    x = np.asarray(inp["x"], f32)
    c = np.asarray(inp["c"], f32)
    for core in range(N_CORES):
        if core in ACTIVE_CORES:
            b = ACTIVE_CORES.index(core)
            xT = np.ascontiguousarray(x[b].T.reshape(DC, 128, S // dm.TQ, dm.TQ).transpose(2, 1, 0, 3))
            m = dict(shared)
            m["x"] = xT
            m["cvec"] = _fm_vec(c[b])
        else:
            m = dict(idle)
            m["x"] = np.zeros((S // dm.TQ, 128, DC, dm.TQ), f32)
            m["cvec"] = np.zeros((128, DC), f32)
        in_maps.append(m)
    return in_maps


def assemble(results, cfg, B):
    dm = Dims(**cfg)
    out = np.empty((B, dm.S, dm.D), np.float32)
    for b in range(B):
        o = np.asarray(results[ACTIVE_CORES[b]]["out"], np.float32).reshape(dm.S // dm.TQ, 128, dm.DC, dm.TQ)
        out[b] = o.transpose(0, 3, 2, 1).reshape(dm.S, dm.D)
    return out


_NC_CACHE = {}


def run(inp, cfg):
    key = tuple(sorted(cfg.items()))
    if key not in _NC_CACHE:
        _NC_CACHE[key] = build_program(cfg)
    nc = _NC_CACHE[key]
    in_maps = prep_inputs(inp, cfg)
    res = run_bass_kernel_spmd(nc, in_maps, core_ids=list(range(N_CORES)))
    return assemble(res.results, cfg, inp["x"].shape[0])


def kernel(**inputs):
    return run(inputs, CFG_FULL)
```
